# Optimizing a Trainium2 kernel written in Bass

```python
import jax, jax.numpy as jnp
from jax import lax
import numpy as np

D_MODEL = 1024
BATCH = 4
SEQ = 8192
DEPTH = 2

GRID_W = 64
CTX_LEN = 256
N_HEADS = 16
N_KV_HEADS = 4
HEAD_DIM = 64
Q_W = N_HEADS * HEAD_DIM
KV_W = N_KV_HEADS * HEAD_DIM
WINDOW = 128
BLK = 128
ROPE_THETA = 10000.0
CHUNK = 128
A_W = D_MODEL
A_GROUPS = 8
A_GW = A_W // A_GROUPS
B_W = D_MODEL
CONV_W = 3
N_BRANCH = 3
BRANCH_W = D_MODEL
D_FF = 2816
EPS = 1e-6
NEG = -1e30
OFF_Q = 0
OFF_K = OFF_Q + Q_W
OFF_V = OFF_K + KV_W
OFF_A = OFF_V + KV_W
OFF_B = OFF_A + 2 * A_W
OFF_G = OFF_B + 3 * B_W
IN_W = OFF_G + N_BRANCH * D_MODEL

kernel_name = "hybrid_gated_parallel_dit_block"


def rmsnorm(x, g):
    xf = x.astype(jnp.float32)
    y = xf * lax.rsqrt(jnp.mean(xf * xf, axis=-1, keepdims=True) + EPS)
    return (y * g.astype(jnp.float32)).astype(x.dtype)


def modulate(h, shift, scale):
    return h * (1.0 + scale) + shift


def dwconv3(x, w):
    ch = x.shape[-1]
    return lax.conv_general_dilated(
        x, w[:, None, :].astype(x.dtype), window_strides=(1,), padding=[(CONV_W // 2, CONV_W // 2)],
        dimension_numbers=("NWC", "WIO", "NWC"), feature_group_count=ch)


def axial_rope_tables(n):
    rows = n // GRID_W
    row = jnp.broadcast_to(jnp.arange(rows, dtype=jnp.float32)[:, None], (rows, GRID_W)).reshape(n)
    col = jnp.broadcast_to(jnp.arange(GRID_W, dtype=jnp.float32)[None, :], (rows, GRID_W)).reshape(n)
    half = HEAD_DIM // 2
    inv = ROPE_THETA ** (-jnp.arange(0, half, 2, dtype=jnp.float32) / half)
    ang = jnp.concatenate([row[:, None] * inv, col[:, None] * inv], axis=-1)
    return jnp.cos(ang), jnp.sin(ang)


def apply_rope(t, cos, sin):
    half = HEAD_DIM // 2
    tf = t.astype(jnp.float32)
    t1, t2 = tf[..., :half], tf[..., half:]
    cs, sn = cos[None, :, None, :], sin[None, :, None, :]
    return jnp.concatenate([t1 * cs - t2 * sn, t1 * sn + t2 * cs], axis=-1).astype(t.dtype)


def latent_attention(q, k, v, kc, vc, sink):
    b, n = q.shape[:2]
    nb = n // BLK
    grp = N_HEADS // N_KV_HEADS
    m = kc.shape[1]
    scale = HEAD_DIM ** -0.5
    qb = q.reshape(b, nb, BLK, N_KV_HEADS, grp, HEAD_DIM)

    def neighbours(t):
        tb = t.reshape(b, nb, BLK, N_KV_HEADS, HEAD_DIM)
        tp = jnp.pad(tb, ((0, 0), (1, 1), (0, 0), (0, 0), (0, 0)))
        return jnp.concatenate([tp[:, :-2], tp[:, 1:-1], tp[:, 2:]], axis=2)

    kn, vn = neighbours(k), neighbours(v)
    rel = (jnp.arange(3 * BLK)[None, :] - BLK) - jnp.arange(BLK)[:, None]
    band = jnp.abs(rel) <= WINDOW
    src_blk = jnp.arange(nb)[:, None] + jnp.arange(3 * BLK)[None, :] // BLK - 1
    in_range = (src_blk >= 0) & (src_blk < nb)
    sink_l = sink.astype(jnp.float32).reshape(N_KV_HEADS, grp)[None, :, :, None, None]

    def block(args):
        qi, ki, vi, ok = args
        s_loc = jnp.einsum("bqkgd,bjkd->bkgqj", qi, ki).astype(jnp.float32) * scale
        s_loc = jnp.where(band & ok[None, :], s_loc, NEG)
        s_ctx = jnp.einsum("bqkgd,bckd->bkgqc", qi, kc).astype(jnp.float32) * scale
        s_snk = jnp.broadcast_to(sink_l, s_ctx.shape[:-1] + (1,))
        p = jax.nn.softmax(jnp.concatenate([s_loc, s_ctx, s_snk], axis=-1), axis=-1).astype(vi.dtype)
        return (jnp.einsum("bkgqj,bjkd->bqkgd", p[..., :3 * BLK], vi)
                + jnp.einsum("bkgqc,bckd->bqkgd", p[..., 3 * BLK:3 * BLK + m], vc))

    xs = (jnp.moveaxis(qb, 1, 0), jnp.moveaxis(kn, 1, 0), jnp.moveaxis(vn, 1, 0), in_range)
    out = lax.map(block, xs)
    return jnp.moveaxis(out, 0, 1).reshape(b, n, Q_W)


def context_attention(qc, kc, vc, sink):
    b, m = qc.shape[:2]
    grp = N_HEADS // N_KV_HEADS
    qg = qc.reshape(b, m, N_KV_HEADS, grp, HEAD_DIM)
    s = jnp.einsum("bqkgd,bckd->bkgqc", qg, kc).astype(jnp.float32) * (HEAD_DIM ** -0.5)
    s_snk = jnp.broadcast_to(sink.astype(jnp.float32).reshape(N_KV_HEADS, grp)[None, :, :, None, None],
                             s.shape[:-1] + (1,))
    p = jax.nn.softmax(jnp.concatenate([s, s_snk], axis=-1), axis=-1).astype(vc.dtype)
    o = jnp.einsum("bkgqc,bckd->bqkgd", p[..., :m], vc)
    return o.reshape(b, m, Q_W)


def chunk_spatial_gating(z_a, w_s, b_s, g_v):
    z = jax.nn.gelu(z_a)
    u, v = z[..., :A_W], z[..., A_W:]
    v = rmsnorm(v, g_v)
    b, n = v.shape[:2]
    vr = v.reshape(b, n // CHUNK, CHUNK, A_GROUPS, A_GW)
    mixed = jnp.einsum("gpq,bnqgc->bnpgc", w_s, vr) + b_s.T[None, None, :, :, None]
    return u * mixed.reshape(b, n, A_W)


def short_conv_mixer(z_b, w_sconv):
    bg, cg, hb = z_b[..., :B_W], z_b[..., B_W:2 * B_W], z_b[..., 2 * B_W:]
    return bg * dwconv3(cg * hb, w_sconv)


def merge_branches(z, y_attn, w_s, b_s, g_v, w_sconv, b_gate, w_branch, w_out):
    y_a = chunk_spatial_gating(z[..., OFF_A:OFF_B], w_s, b_s, g_v)
    y_b = short_conv_mixer(z[..., OFF_B:OFF_G], w_sconv)
    gates = jax.nn.sigmoid(z[..., OFF_G:] + b_gate)
    g_c, g_a, g_b = gates[..., :D_MODEL], gates[..., D_MODEL:2 * D_MODEL], gates[..., 2 * D_MODEL:]
    merged = (g_c * (y_attn @ w_branch[0]) + g_a * (y_a @ w_branch[1]) + g_b * (y_b @ w_branch[2]))
    return merged @ w_out


def conv_ffn(h, w_up, w_fconv, w_down):
    up = h @ w_up
    a, g = up[..., :D_FF], up[..., D_FF:]
    return (jax.nn.silu(dwconv3(a, w_fconv)) * g) @ w_down


def setup_inputs(seed: int = 0) -> dict:
    key = jax.random.key(seed)
    ks = jax.random.split(key, 24)
    f32 = jnp.float32

    def nrm(k, shape, scale):
        return jax.random.normal(k, shape, f32) * scale

    return {
        "x": nrm(ks[0], (BATCH, SEQ, D_MODEL), 1.0),
        "c": nrm(ks[1], (BATCH, D_MODEL), 1.0),
        "ctx": nrm(ks[2], (BATCH, CTX_LEN, D_MODEL), 1.0),
        "c_ctx": nrm(ks[3], (D_MODEL,), 1.0),
        "w_mod": nrm(ks[4], (DEPTH, D_MODEL, 6 * D_MODEL), D_MODEL ** -0.5),
        "b_mod": nrm(ks[5], (DEPTH, 6 * D_MODEL), 0.02),
        "g_mix": 1.0 + nrm(ks[6], (DEPTH, D_MODEL), 0.02),
        "w_in": nrm(ks[7], (DEPTH, D_MODEL, IN_W), D_MODEL ** -0.5),
        "b_gate": nrm(ks[8], (DEPTH, N_BRANCH * D_MODEL), 0.02),
        "sink": nrm(ks[9], (DEPTH, N_HEADS), 0.5),
        "w_spatial": nrm(ks[10], (DEPTH, A_GROUPS, CHUNK, CHUNK), CHUNK ** -0.5),
        "b_spatial": 1.0 + nrm(ks[11], (DEPTH, A_GROUPS, CHUNK), 0.02),
        "g_v": 1.0 + nrm(ks[12], (DEPTH, A_W), 0.02),
        "w_sconv": nrm(ks[13], (DEPTH, CONV_W, B_W), CONV_W ** -0.5),
        "w_branch": nrm(ks[14], (DEPTH, N_BRANCH, BRANCH_W, D_MODEL), BRANCH_W ** -0.5),
        "w_out": nrm(ks[15], (DEPTH, D_MODEL, D_MODEL), D_MODEL ** -0.5),
        "g_ffn": 1.0 + nrm(ks[16], (DEPTH, D_MODEL), 0.02),
        "w_up": nrm(ks[17], (DEPTH, D_MODEL, 2 * D_FF), D_MODEL ** -0.5),
        "w_fconv": nrm(ks[18], (DEPTH, CONV_W, D_FF), CONV_W ** -0.5),
        "w_down": nrm(ks[19], (DEPTH, D_FF, D_MODEL), D_FF ** -0.5),
        "g_final": 1.0 + nrm(ks[20], (D_MODEL,), 0.02),
    }


def reference(x, c, ctx, c_ctx, w_mod, b_mod, g_mix, w_in, b_gate, sink, w_spatial, b_spatial, g_v,
              w_sconv, w_branch, w_out, g_ffn, w_up, w_fconv, w_down, g_final):
    b, n, _ = x.shape
    m = ctx.shape[1]
    cos, sin = axial_rope_tables(n)
    xc = ctx
    for l in range(DEPTH):
        last = l == DEPTH - 1
        mod = jax.nn.silu(c) @ w_mod[l] + b_mod[l]
        mod_c = jax.nn.silu(c_ctx) @ w_mod[l] + b_mod[l]
        sh1, sc1, gt1, sh2, sc2, gt2 = [t[:, None, :] for t in jnp.split(mod, 6, axis=-1)]
        csh1, csc1, cgt1, csh2, csc2, cgt2 = jnp.split(mod_c, 6, axis=-1)

        h = modulate(rmsnorm(x, g_mix[l]), sh1, sc1)
        hc = modulate(rmsnorm(xc, g_mix[l]), csh1, csc1)
        z = h @ w_in[l]
        q = apply_rope(z[..., OFF_Q:OFF_K].reshape(b, n, N_HEADS, HEAD_DIM), cos, sin)
        k = apply_rope(z[..., OFF_K:OFF_V].reshape(b, n, N_KV_HEADS, HEAD_DIM), cos, sin)
        v = z[..., OFF_V:OFF_A].reshape(b, n, N_KV_HEADS, HEAD_DIM)
        if last:
            zc = hc @ w_in[l][:, OFF_K:OFF_A]
            kc = zc[..., :KV_W].reshape(b, m, N_KV_HEADS, HEAD_DIM)
            vc = zc[..., KV_W:].reshape(b, m, N_KV_HEADS, HEAD_DIM)
        else:
            zc = hc @ w_in[l]
            kc = zc[..., OFF_K:OFF_V].reshape(b, m, N_KV_HEADS, HEAD_DIM)
            vc = zc[..., OFF_V:OFF_A].reshape(b, m, N_KV_HEADS, HEAD_DIM)
        y_attn = latent_attention(q, k, v, kc, vc, sink[l])
        x = x + gt1 * merge_branches(z, y_attn, w_spatial[l], b_spatial[l], g_v[l], w_sconv[l],
                                     b_gate[l], w_branch[l], w_out[l])
        h2 = modulate(rmsnorm(x, g_ffn[l]), sh2, sc2)
        x = x + gt2 * conv_ffn(h2, w_up[l], w_fconv[l], w_down[l])

        if not last:
            qc = zc[..., OFF_Q:OFF_K].reshape(b, m, N_HEADS, HEAD_DIM)
            yc_attn = context_attention(qc, kc, vc, sink[l])
            xc = xc + cgt1 * merge_branches(zc, yc_attn, w_spatial[l], b_spatial[l], g_v[l], w_sconv[l],
                                            b_gate[l], w_branch[l], w_out[l])
            hc2 = modulate(rmsnorm(xc, g_ffn[l]), csh2, csc2)
            xc = xc + cgt2 * conv_ffn(hc2, w_up[l], w_fconv[l], w_down[l])
    return rmsnorm(x, g_final)
```

```python
import numpy as np
import ml_dtypes
import concourse.bass as bass
import concourse.mybir as mybir
from concourse.bass_utils import run_bass_kernel_spmd

F32 = mybir.dt.float32
BF16 = mybir.dt.bfloat16
AF = mybir.ActivationFunctionType
ALU = mybir.AluOpType

D = 1024
KC = 8
SEQ = 8192
HALF = 4096
NCTX = 256
DFF = 2816
NJ = 22
OFF_Q, OFF_K, OFF_V, OFF_A, OFF_B, OFF_G = 0, 1024, 1280, 1536, 3584, 6656
IN_W = 9728
EPS = 1e-6
TILE = 512
_DBG_LABEL = [None]
_DBG_MM = []
PB = [35, 34, 33, 32]
T0 = 36 * 128
SLOT = 4096
NSLOT = 4
PREFETCH = 2


def _slab(wmat, rows, cols):
    sub = wmat[np.asarray(rows)[:, None], np.asarray(cols)[None, :]]
    kt, wd = sub.shape
    return np.ascontiguousarray(sub.reshape(kt // 128, 128, wd).transpose(1, 0, 2)).reshape(128, -1)


def _tape_plan():
    slabs = []
    for m in range(2):
        slabs.append((("k", m), 128, 8))
    slabs.append((("v",), 256, 8))
    for m in range(2):
        for jj in range(4):
            slabs.append((("q", m, jj), 128, 8))
    for cc in range(8):
        slabs.append((("cg", cc), 128, 8))
        slabs.append((("hb", cc), 128, 8))
        slabs.append((("bg", cc), 128, 8))
    for gg in range(8):
        slabs.append((("au", gg), 128, 8))
    for hh in range(2):
        slabs.append((("av", hh), 512, 8))
    for oc in range(8):
        for i in range(3):
            slabs.append((("br", i, oc), 128, 8))
            slabs.append((("gt", i, oc), 128, 8))
    for oc in range(8):
        slabs.append((("wo", oc), 128, 8))
    n_mixer = len(slabs)
    for jj in range(NJ):
        slabs.append((("ua", jj), 128, 8))
        slabs.append((("ug", jj), 128, 8))
    for oc in range(8):
        slabs.append((("dn", oc), 128, NJ))
    offs = {}
    segs = []
    cur = None
    off = 0
    for idx, (nm, wd, kcn) in enumerate(slabs):
        ln = wd * kcn
        if cur is None or cur[1] + ln > SLOT or idx == n_mixer:
            cur = [off, 0, []]
            segs.append(cur)
        offs[nm] = (len(segs) - 1, cur[1], wd, kcn)
        cur[1] += ln
        cur[2].append(nm)
        off += ln
    seg_mixer = offs[("ua", 0)][0]
    return slabs, offs, segs, off, seg_mixer


_SLABS, _OFFS, _SEGS, _LT, _SEG_FFN0 = _tape_plan()


def _head_cols(base, head):
    return list(range(base + head * 64, base + head * 64 + 64))


def _swap64(cols):
    return [cols[(d + 32) % 64] for d in range(64)]


def _build_tape(w_in, w_branch, w_out, w_up, w_down):
    tape = np.empty((128, _LT), np.float32)
    allr = np.arange(1024)
    off = 0
    for (nm, wd, kcn) in _SLABS:
        kind = nm[0]
        if kind in ("k", "ks"):
            m = nm[1]
            c0 = _head_cols(OFF_K, 2 * m)
            c1 = _head_cols(OFF_K, 2 * m + 1)
            if kind == "ks":
                c0, c1 = _swap64(c0), _swap64(c1)
            s = _slab(w_in, allr, c0 + c1)
        elif kind == "v":
            s = _slab(w_in, allr, range(OFF_V, OFF_V + 256))
        elif kind in ("q", "qs"):
            m, jj = nm[1], nm[2]
            c0 = _head_cols(OFF_Q, 4 * (2 * m) + jj)
            c1 = _head_cols(OFF_Q, 4 * (2 * m + 1) + jj)
            if kind == "qs":
                c0, c1 = _swap64(c0), _swap64(c1)
            s = _slab(w_in, allr, c0 + c1)
        elif kind == "au":
            s = _slab(w_in, allr, range(OFF_A + nm[1] * 128, OFF_A + nm[1] * 128 + 128))
        elif kind == "av":
            s = _slab(w_in, allr, range(OFF_A + 1024 + nm[1] * 512, OFF_A + 1024 + nm[1] * 512 + 512))
        elif kind == "bg":
            s = _slab(w_in, allr, range(OFF_B + nm[1] * 128, OFF_B + nm[1] * 128 + 128))
        elif kind == "cg":
            s = _slab(w_in, allr, range(OFF_B + 1024 + nm[1] * 128, OFF_B + 1024 + nm[1] * 128 + 128))
        elif kind == "hb":
            s = _slab(w_in, allr, range(OFF_B + 2048 + nm[1] * 128, OFF_B + 2048 + nm[1] * 128 + 128))
        elif kind == "gt":
            i, oc = nm[1], nm[2]
            s = _slab(w_in, allr, range(OFF_G + i * 1024 + oc * 128, OFF_G + i * 1024 + oc * 128 + 128))
        elif kind == "br":
            i, oc = nm[1], nm[2]
            if i == 0:
                rows = []
                for m in range(2):
                    for jj in range(4):
                        rows += _head_cols(0, 4 * (2 * m) + jj) + _head_cols(0, 4 * (2 * m + 1) + jj)
            else:
                rows = allr
            s = _slab(w_branch[i], rows, range(oc * 128, oc * 128 + 128))
        elif kind == "wo":
            s = _slab(w_out, allr, range(nm[1] * 128, nm[1] * 128 + 128))
        elif kind == "ua":
            s = _slab(w_up, allr, range(nm[1] * 128, nm[1] * 128 + 128))
        elif kind == "ug":
            s = _slab(w_up, allr, range(DFF + nm[1] * 128, DFF + nm[1] * 128 + 128))
        elif kind == "dn":
            s = _slab(w_down, np.arange(DFF), range(nm[1] * 128, nm[1] * 128 + 128))
        else:
            raise AssertionError(nm)
        ln = wd * kcn
        assert s.shape[1] == ln
        tape[:, off:off + ln] = s
        off += ln
    return tape


def _perm_matrix():
    pm = np.zeros((128, 128), np.float32)
    for mcol in range(128):
        pm[64 * (mcol // 64) + (mcol % 64 + 32) % 64, mcol] = 1.0
    return pm.astype(ml_dtypes.bfloat16)


def _vecT(v, n):
    return np.ascontiguousarray(np.asarray(v, np.float32).reshape(n, 128).T)


def _rope_tables(pos):
    pos = np.asarray(pos)
    row = (pos // 64).astype(np.float32)
    col = (pos % 64).astype(np.float32)
    inv = (np.float32(10000.0) ** (-np.arange(0, 32, 2, dtype=np.float32) / np.float32(32))).astype(np.float32)
    ang = np.concatenate([row[:, None] * inv, col[:, None] * inv], axis=-1).astype(np.float32)
    cos = np.cos(ang).astype(np.float32).T
    sin = np.sin(ang).astype(np.float32).T
    cosT = np.tile(cos, (4, 1))
    sinT = np.tile(np.concatenate([-sin, sin], axis=0), (2, 1))
    return np.ascontiguousarray(cosT), np.ascontiguousarray(sinT)


class _Op:
    __slots__ = ("eng", "fn", "deps", "sig", "stream", "ordn", "val", "vc", "dma")


class _Cell:
    __slots__ = ("writer", "readers")

    def __init__(self):
        self.writer = None
        self.readers = {}


class Sched:
    ENGS = ("pe", "act", "dve", "pool", "sp")

    def __init__(self):
        self.ops = []
        self.cells = {}
        self.stream_count = {}
        self.last_dma = {}

    def add(self, eng, fn, reads=(), writes=(), dma=None):
        op = _Op()
        op.eng = eng
        op.fn = fn
        op.dma = dma
        op.stream = ("dma", dma) if dma is not None else ("eng", eng)
        op.sig = dma is not None
        op.val = 0
        op.vc = None
        n = self.stream_count.get(op.stream, 0) + 1
        self.stream_count[op.stream] = n
        op.ordn = n
        deps = {}

        def dep(o):
            if o is None or o is op:
                return
            if eng == "pe" and o.eng == "pe" and o.dma is None and dma is None:
                return
            cur = deps.get(o.stream)
            if cur is None or cur.ordn < o.ordn:
                deps[o.stream] = o

        cells = self.cells
        for r in reads:
            c = cells.get(r)
            if c is None:
                c = cells[r] = _Cell()
            dep(c.writer)
        for w in writes:
            c = cells.get(w)
            if c is None:
                c = cells[w] = _Cell()
            dep(c.writer)
            for o in c.readers.values():
                dep(o)
        if dma is not None:
            dep(self.last_dma.get(dma))
            self.last_dma[dma] = op
        for r in reads:
            cells[r].readers[op.stream] = op
        for w in writes:
            c = cells[w]
            c.writer = op
            c.readers = {}
        for o in deps.values():
            o.sig = True
        op.deps = list(deps.values())
        self.ops.append(op)
        return op

    def emit(self, nc):
        streams = sorted(self.stream_count.keys(), key=str)
        sidx = {s: i for i, s in enumerate(streams)}
        ns = len(streams)
        import contextlib
        stack = contextlib.ExitStack()
        sems = {}
        for s in streams:
            sems[s] = stack.enter_context(nc.semaphore("s_" + "_".join(str(x) for x in s)))
        know = {e: np.zeros(ns, np.int64) for e in self.ENGS}
        counts = {s: 0 for s in streams}
        prog = {e: [] for e in self.ENGS}
        for op in self.ops:
            k = know[op.eng]
            lst = prog[op.eng]
            for d in op.deps:
                si = sidx[d.stream]
                if k[si] < d.val:
                    lst.append((0, sems[d.stream], int(d.val)))
                    np.maximum(k, d.vc, out=k)
            if op.sig:
                inc = 16 if op.dma is not None else 1
                counts[op.stream] += inc
                op.val = counts[op.stream]
                vc = k.copy()
                vc[sidx[op.stream]] = op.val
                op.vc = vc
                lst.append((1, op.fn, sems[op.stream], inc))
            else:
                lst.append((1, op.fn, None, 0))
        for s in streams:
            if s[0] == "dma" and counts[s] > 0:
                prog["sp"].append((0, sems[s], counts[s]))
        self.ops = None
        self.cells = None

        def run(engh, lst):
            for it in lst:
                if it[0] == 0:
                    engh.wait_ge(it[1], it[2])
                else:
                    ins = it[1](engh)
                    if it[2] is not None:
                        ins.then_inc(it[2], it[3])

        with stack:
            with nc.Block() as block:
                @block.tensor
                def _(e):
                    run(e, prog["pe"])

                @block.scalar
                def _(e):
                    run(e, prog["act"])

                @block.vector
                def _(e):
                    run(e, prog["dve"])

                @block.gpsimd
                def _(e):
                    run(e, prog["pool"])

                @block.sync
                def _(e):
                    run(e, prog["sp"])


class Ring:
    def __init__(self, name, bufs):
        self.name = name
        self.bufs = bufs
        self.i = -1

    def next(self):
        self.i = (self.i + 1) % len(self.bufs)
        return self.bufs[self.i], (self.name, self.i)


def build_program(debug=False, upto=None):
    import contextlib
    nc = bass.Bass("TRN2", target_bir_lowering=False)
    S = Sched()
    es = contextlib.ExitStack()

    def dram(name, shape, dt, kind):
        return nc.dram_tensor(name, list(shape), dt, kind=kind).ap()

    def sb(name, shape, dt):
        return es.enter_context(nc.sbuf_tensor(name, list(shape), dt))

    d_x0 = dram("xT", [8, 128, T0], F32, "ExternalInput")
    d_c0 = dram("ctxT", [8, 128, NCTX], F32, "ExternalInput")
    d_cvec = dram("cvec", [128, 8, 2], F32, "ExternalInput")
    d_wmod = dram("w_mod", [2, 1024, 6144], F32, "ExternalInput")
    d_bmod = dram("bmodT", [2, 128, 48], F32, "ExternalInput")
    d_gmix = dram("gmixT", [2, 128, 8], F32, "ExternalInput")
    d_gffn = dram("gffnT", [2, 128, 8], F32, "ExternalInput")
    d_gfin = dram("gfinT", [128, 8], F32, "ExternalInput")
    d_bgate = dram("bgateT", [2, 128, 24], F32, "ExternalInput")
    d_sink = dram("sinkx", [1, 32], F32, "ExternalInput")
    d_ws = dram("wsT", [2, 128, 1024], F32, "ExternalInput")
    d_bs = dram("bsrow", [2, 1, 1024], F32, "ExternalInput")
    d_gvb = dram("gvb", [2, 128, 1024], F32, "ExternalInput")
    d_wsc = dram("wsconvT", [2, 128, 24], F32, "ExternalInput")
    d_wfc = dram("wfconvT", [2, 128, 66], F32, "ExternalInput")
    d_cos = dram("cosT", [128, T0], F32, "ExternalInput")
    d_sin = dram("sinT", [128, T0], F32, "ExternalInput")
    d_maskp = dram("maskP", [128, 128], BF16, "ExternalInput")
    d_maskn = dram("maskN", [128, 128], BF16, "ExternalInput")
    d_perm = dram("permM", [128, 128], BF16, "ExternalInput")
    d_tape = dram("tape32", [2, 128, _LT], F32, "ExternalInput")
    d_out = dram("outT", [8, 128, HALF], F32, "ExternalOutput")
    d_t16 = dram("tape16", [2, 128, _LT], BF16, "Internal")
    d_xm1 = dram("xm1", [8, 128, PB[0] * 128], F32, "Internal")
    d_x1 = dram("x1", [8, 128, PB[1] * 128], F32, "Internal")
    d_xm2 = dram("xm2", [8, 128, PB[2] * 128], F32, "Internal")
    d_cm1 = dram("cm1", [8, 128, NCTX], F32, "Internal")
    d_c1 = dram("c1", [8, 128, NCTX], F32, "Internal")
    dbg = {}
    if debug:
        for nm, shp in (("d_xm1", [8, 128, PB[0] * 128]), ("d_x1", [8, 128, PB[1] * 128]),
                        ("d_xm2", [8, 128, PB[2] * 128]), ("d_cm1", [8, 128, NCTX]), ("d_c1", [8, 128, NCTX])):
            dbg[nm] = dram(nm, shp, F32, "ExternalOutput")

    def pk(ap):
        return ap.rearrange("k p t -> p k t")

    wring = [sb(f"wr{i}", [128, SLOT], BF16) for i in range(NSLOT)]
    xt = [sb("xt_a", [128, 8, 641], F32), sb("xt_b", [128, 8, 641], F32)]
    hTs = [sb("hT_a", [128, 8, 642], BF16), sb("hT_b", [128, 8, 642], BF16)]
    sqring = Ring("sq", [sb(f"sq{i}", [128, 641], BF16) for i in range(2)])
    n32 = Ring("n32", [sb(f"n32_{i}", [128, 641], F32) for i in range(2)])
    rs_buf = sb("rs_buf", [128, 641], F32)
    rstd_buf = sb("rstd_buf", [128, 641], F32)
    p32 = Ring("p32", [sb(f"p32_{i}", [128, 514], F32) for i in range(7)])
    p16 = Ring("p16", [sb(f"p16_{i}", [128, 512], BF16) for i in range(4)])
    cs_t = sb("cs_t", [128, 640], F32)
    sn_t = sb("sn_t", [128, 640], F32)
    qbuf = sb("qbuf", [128, 8, 512], BF16)
    kbuf = [sb(f"kbuf{i}", [128, 2, 512], BF16) for i in range(2)]
    k0buf = sb("k0buf", [128, 2, 128], BF16)
    vbuf = [sb(f"vbuf{i}", [128, 4, 4, 128], BF16) for i in range(2)]
    v0buf = sb("v0buf", [128, 4, 128], BF16)
    kcbuf = sb("kcbuf", [128, 2, 256], BF16)
    vcbuf = sb("vcbuf", [128, 2, 4, 128], BF16)
    ybig = sb("ybig", [128, 24, 512], BF16)
    uT = sb("uT", [128, 8, 512], BF16)
    gvr = Ring("gv", [sb(f"gv{i}", [128, 1024], BF16) for i in range(2)])
    vnr = Ring("vn", [sb(f"vn{i}", [128, 1024], BF16) for i in range(2)])
    mcarry = sb("mcarry", [128, 8, 2], F32)
    acarry = sb("acarry", [128, NJ, 2], F32)
    small = sb("small", [128, 16], F32)
    mhalf = sb("mhalf", [128, 1], F32)
    ones_bf = sb("ones_bf", [128, 128], BF16)
    sinkl = sb("sinkl", [1, 2, 128], BF16)
    maskp = sb("maskp", [128, 128], BF16)
    maskn = sb("maskn", [128, 128], BF16)
    permm = sb("permm", [128, 128], BF16)
    esink = sb("esink", [1, 2048], BF16)
    sink32 = sb("sink32", [1, 32], F32)
    wsT = sb("wsTb", [128, 1024], BF16)
    bsrow = sb("bsrowb", [1, 1024], BF16)
    gvb = sb("gvbs", [128, 1024], BF16)
    wsc = sb("wscs", [128, 2, 24], F32)
    wfc = sb("wfcs", [128, 2, 66], F32)
    bgate = sb("bgates", [128, 2, 24], F32)
    gmix = sb("gmixs", [128, 2, 8], F32)
    gffn = sb("gffns", [128, 2, 8], F32)
    gfin = sb("gfins", [128, 8], F32)
    bmod = sb("bmods", [128, 2, 48], F32)
    cvec = sb("cvecs", [128, 8, 2], F32)
    scv = sb("scv", [128, 8, 2], F32)
    modv = sb("modv", [128, 2, 48, 2], F32)
    a1v = sb("a1v", [128, 2, 8, 2], F32)
    a2v = sb("a2v", [128, 2, 8, 2], F32)

    ps_pairs = [es.enter_context(nc.psum_tensor(f"ps{i}", [128, 1024], F32)) for i in range(4)]
    ps_banks = [ps_pairs[i // 2][:, (i % 2) * 512:(i % 2) * 512 + 512] for i in range(8)]

    class PsRing:
        def __init__(self, banks):
            self.all = list(banks)
            self.banks = list(banks)
            self.k = -1

        def restrict(self, banks):
            self.banks = list(banks) if banks is not None else list(self.all)
            self.k = -1

        def _check(self, b):
            c = S.cells.get(("ps", b))
            assert c is None or c.writer is None or len(c.readers) > 0, f"psum bank {b} re-allocated while open"

        def next(self):
            self.k = (self.k + 1) % len(self.banks)
            b = self.banks[self.k]
            self._check(b)
            return ps_banks[b], ("ps", b)

        def next2(self):
            for _ in range(len(self.banks)):
                self.k = (self.k + 1) % len(self.banks)
                b = self.banks[self.k]
                if b % 2 == 0 and (b + 1) in self.banks:
                    break
            else:
                raise AssertionError("no pair")
            self._check(b)
            self._check(b + 1)
            self.k = self.banks.index(b + 1)
            return ps_pairs[b // 2], [("ps", b), ("ps", b + 1)]

    psr = PsRing(range(8))
    ps_att_st = PsRing([0, 1])
    psr_norm = PsRing([0, 1])
    ps_fast = PsRing([2, 3])
    ps_slow = PsRing([4, 5, 6, 7])
    ps_att_ot = PsRing([2, 3])

    def XT(par):
        return [("xt", par, kc) for kc in range(8)]

    def OP(eng, name, *args, reads=(), writes=(), **kw):
        def fn(e, name=name, args=args, kw=kw):
            return getattr(e, name)(*args, **kw)
        return S.add(eng, fn, reads, writes)

    def DMA(eng, slot, out, in_, reads=(), writes=()):
        def fn(e, out=out, in_=in_):
            return e.dma_start(out=out, in_=in_)
        return S.add(eng, fn, reads, writes, dma=slot)

    def MM(ps, lhsT, rhs, start, stop, reads, writes):
        _DBG_MM.append((_DBG_LABEL[0], int(np.prod(rhs.shape[1:])), bool(start), str(writes[0][0]), str(reads[0][0])))
        def fn(e, ps=ps, lhsT=lhsT, rhs=rhs, start=start, stop=stop):
            return e.matmul(ps, lhsT=lhsT, rhs=rhs, start=start, stop=stop)
        return S.add("pe", fn, reads, writes)

    class Tape:
        def __init__(self):
            self.plan = []
            self.issued = 0
            self.pos = -1
            self.force = False

        def new_tile(self):
            self.force = True

        def extend(self, L, seg_lo, seg_hi):
            for sg in range(seg_lo, seg_hi):
                self.plan.append((L, sg))

        def _issue(self, k):
            L, sg = self.plan[k]
            off, ln, _ = _SEGS[sg]
            slot = k % NSLOT
            DMA("sp", ("w", slot), wring[slot][:, 0:ln], d_t16[L, :, off:off + ln],
                reads=[("t16", L, sg)], writes=[("w", slot)])

        def get(self, L, name, hold_prev=False):
            _DBG_LABEL[0] = (L, name)
            sg, soff, wd, kcn = _OFFS[name]
            if self.pos < 0 or self.force or self.plan[self.pos] != (L, sg):
                self.force = False
                self.pos += 1
                assert self.plan[self.pos] == (L, sg), (self.plan[self.pos], L, sg, name)
            while self.issued < min(len(self.plan), self.pos + (PREFETCH if hold_prev else PREFETCH + 1) + 1):
                self._issue(self.issued)
                self.issued += 1
            slot = self.pos % NSLOT
            view = wring[slot][:, soff:soff + wd * kcn].rearrange("p (k c) -> p k c", k=kcn)
            return view, ("w", slot)

    tape = Tape()

    conv_state = {"n": 0}

    def convert_seg(L, sg):
        off, ln, _ = _SEGS[sg]
        n = conv_state["n"]
        conv_state["n"] += 1
        DMA("pool", ("cv", n % 8), d_t16[L, :, off:off + ln], d_tape[L, :, off:off + ln],
            reads=[], writes=[("t16", L, sg)])

    ld = {"n": 0}

    def LOAD(out, in_, cell):
        n = ld["n"]
        ld["n"] += 1
        DMA("sp", ("ld", n), out, in_, reads=[], writes=[cell])

    LOAD(maskp[:], d_maskp, ("maskp",))
    LOAD(maskn[:], d_maskn, ("maskn",))
    LOAD(permm[:], d_perm, ("permm",))
    LOAD(cvec[:], d_cvec, ("cvec",))
    LOAD(gfin[:], d_gfin, ("gfin",))
    LOAD(sink32[:], d_sink, ("sink32",))
    for L in range(2):
        LOAD(bmod[:, L, :], d_bmod[L], ("bmod", L))
        LOAD(gmix[:, L, :], d_gmix[L], ("gmix", L))
        LOAD(gffn[:, L, :], d_gffn[L], ("gffn", L))
        LOAD(bgate[:, L, :], d_bgate[L], ("bgate", L))
        LOAD(wsc[:, L, :], d_wsc[L], ("wsc", L))
        LOAD(wfc[:, L, :], d_wfc[L], ("wfc", L))
    OP("pool", "memset", ones_bf[:], 1.0, writes=[("ones",)])
    OP("pool", "memset", mhalf[:], -0.5, writes=[("mhalf",)])
    OP("pool", "memset", sinkl[:, 0, 0:64], 0.0, writes=[("sinkl",)])
    OP("pool", "memset", sinkl[:, 0, 64:128], 1.0, writes=[("sinkl",)])
    OP("pool", "memset", sinkl[:, 1, 0:64], 1.0, writes=[("sinkl",)])
    OP("pool", "memset", sinkl[:, 1, 64:128], 0.0, writes=[("sinkl",)])
    for i in range(2):
        OP("pool", "memset", vbuf[i][:], 1.0, writes=[("v", i)])
    OP("pool", "memset", v0buf[:], 1.0, writes=[("v0",)])
    OP("pool", "memset", vcbuf[:], 1.0, writes=[("vc",)])

    def load_layer_consts(L):
        DMA("pool", ("lc", 0), wsT[:], d_ws[L], reads=[], writes=[("wsT",)])
        DMA("pool", ("lc", 1), bsrow[:], d_bs[L], reads=[], writes=[("bsrow",)])
        DMA("pool", ("lc", 2), gvb[:], d_gvb[L], reads=[], writes=[("gvb", L)])
        OP("act", "activation", esink[0:1, :].rearrange("p (h q) -> p h q", h=16),
           sink32[0:1, L * 16:(L + 1) * 16].unsqueeze(2).broadcast_to([1, 16, 128]), AF.Exp,
           reads=[("sink32",)], writes=[("esink", L)])

    load_layer_consts(0)
    for sg in range(len(_SEGS)):
        convert_seg(0, sg)

    OP("act", "activation", scv[:], cvec[:], AF.Silu, reads=[("cvec",)], writes=[("scv",)])
    for L in range(2):
        mps, mcell = psr.next()
        for pc in range(12):
            par = (pc + 1) % 2 if False else pc % 2
            stage = xt[par][:].rearrange("p k t -> p (k t)")[:, 0:4096].rearrange("p (k c) -> p k c", k=8)
            src = d_wmod[L].rearrange("(k p) c -> p k c", p=128)[:, :, pc * 512:(pc + 1) * 512]
            DMA("sp", ("xlp", par), stage, src, reads=[], writes=XT(par))
            for c4 in range(4):
                ch = pc * 4 + c4
                for kc in range(8):
                    MM(mps[:, ch * 2:ch * 2 + 2], stage[:, kc, c4 * 128:(c4 + 1) * 128], scv[:, kc, :],
                       kc == 0, kc == 7, reads=XT(par) + [("scv",)], writes=[mcell])
        bm = bmod[:, L, :].unsqueeze(2).broadcast_to([128, 48, 2])
        OP("dve", "tensor_tensor", modv[:, L, :, :], mps[:, 0:96].rearrange("p (c v) -> p c v", v=2), bm,
           ALU.add, reads=[mcell, ("bmod", L)], writes=[("modv", L)])
        OP("dve", "scalar_tensor_tensor", a1v[:, L, :, :], modv[:, L, 8:16, :], 1.0,
           gmix[:, L, :].unsqueeze(2).broadcast_to([128, 8, 2]), ALU.add, ALU.mult,
           reads=[("modv", L), ("gmix", L)], writes=[("a1v", L)])
        OP("dve", "scalar_tensor_tensor", a2v[:, L, :, :], modv[:, L, 32:40, :], 1.0,
           gffn[:, L, :].unsqueeze(2).broadcast_to([128, 8, 2]), ALU.add, ALU.mult,
           reads=[("modv", L), ("gffn", L)], writes=[("a2v", L)])

    def modvec(L, which, kc, v):
        if which == "a1":
            return a1v[:, L, kc, v:v + 1]
        if which == "a2":
            return a2v[:, L, kc, v:v + 1]
        base = {"sh1": 0, "gt1": 16, "sh2": 24, "gt2": 40}[which]
        return modv[:, L, base + kc, v:v + 1]

    def Y(chunk, js=(0, 1, 2, 3), hs=(0, 1)):
        return [("Y", chunk, j, h) for j in js for h in hs]

    def modcells(L):
        return [("modv", L), ("a1v", L), ("a2v", L)]

    small_i = {"n": 0}

    def small_next():
        small_i["n"] = (small_i["n"] + 1) % 16
        i = small_i["n"]
        return small[:, i:i + 1], ("small", i)

    def norm(L, par, ncol, avec, shvec, v, mcs, hT, hk):
        groups = [(0, min(512, ncol))]
        if ncol > 512:
            groups.append((512, ncol - 512))
        pss = [psr_norm.next() for _ in groups]
        for kc in range(8):
            sq, sqc = sqring.next()
            OP("act", "activation", sq[:, 0:ncol], xt[par][:, kc, 0:ncol], AF.Square,
               reads=[("xt", par, kc)], writes=[sqc])
            yield "norm_step"
            for (c0, w), (pb, pc) in zip(groups, pss):
                MM(pb[:, 0:w], ones_bf[:], sq[:, c0:c0 + w], kc == 0, kc == 7,
                   reads=[sqc, ("ones",)], writes=[pc])
        rs, rsc = rs_buf, ("rs",)
        for (c0, w), (pb, pc) in zip(groups, pss):
            OP("act", "activation", rs[:, c0:c0 + w], pb[:, 0:w], AF.Sqrt, bias=EPS, scale=1.0 / 1024,
               reads=[pc], writes=[rsc])
        rstd, rstdc = rstd_buf, ("rstd",)
        OP("dve", "reciprocal", rstd[:, 0:ncol], rs[:, 0:ncol], reads=[rsc], writes=[rstdc])
        yield "norm_step"
        for kc in range(8):
            tmp, tc_ = n32.next()
            OP("dve", "scalar_tensor_tensor", tmp[:, 0:ncol], xt[par][:, kc, 0:ncol], avec(kc), rstd[:, 0:ncol],
               ALU.mult, ALU.mult, reads=[("xt", par, kc), rstdc] + mcs, writes=[tc_])
            OP("act", "activation", hT[:, kc, 0:ncol], tmp[:, 0:ncol], AF.Identity, bias=shvec(kc), scale=1.0,
               reads=[tc_] + mcs, writes=[("hT", hk, kc)])
            yield "norm_step"

    def proj_fm(L, name, c0, n, hT, hk, ring=None):
        wv, wc = tape.get(L, name)
        kcn = wv.shape[1]
        pb, pc = (ring or psr).next()
        for kc in range(kcn):
            MM(pb[:, 0:n], wv[:, kc, :], hT[:, kc, c0:c0 + n], kc == 0, kc == kcn - 1,
               reads=[wc, ("hT", hk, kc)], writes=[pc])
        return pb, pc, wv, wc

    def load_x(par, dst_c0, src, src_name, t0, w):
        tl = list(range(t0 // TILE, (t0 + w - 1) // TILE + 1))
        for kc in range(8):
            DMA("pool", ("xl", par, kc % 4), xt[par][:, kc, dst_c0:dst_c0 + w], src[kc, :, t0:t0 + w],
                reads=[(src_name, t, kc) for t in tl], writes=[("xt", par, kc)])
            yield "load_step"

    def mixer_tile(L, i, nblk, src, src_name, dst, dst_name, is_ctx, first_of_stage, dbg_dst=None):
        N = 128 * nblk
        s = TILE * i
        v = 1 if is_ctx else 0
        par = cur_par[0]
        hT, hk = hTs[par], par
        mcs = modcells(L)
        if is_ctx:
            ncol = N + 2
            OP("pool", "memset", xt[par][:, :, 0:1], 0.0, writes=XT(par))
            OP("pool", "memset", xt[par][:, :, N + 1:N + 2], 0.0, writes=XT(par))
            yield from load_x(par, 1, src, src_name, 0, N)
        else:
            ncol = N + 129
            if i == 0:
                OP("pool", "memset", xt[par][:, :, 0:1], 0.0, writes=XT(par))
                yield from load_x(par, 1, src, src_name, 0, ncol - 1)
            else:
                yield from load_x(par, 0, src, src_name, s - 1, ncol)
        yield "loads"
        if not is_ctx:
            DMA("pool", ("rope", 0), cs_t[:, 0:N + 128], d_cos[:, s:s + N + 128], reads=[], writes=[("cs",)])
            DMA("pool", ("rope", 1), sn_t[:, 0:N + 128], d_sin[:, s:s + N + 128], reads=[], writes=[("sn",)])
        for _ in norm(L, par, ncol, lambda kc: modvec(L, "a1", kc, v), lambda kc: modvec(L, "sh1", kc, v), v, mcs, hT, hk):
            yield "norm_step"
        yield "head"
        tape.new_tile()

        kvpar = i % 2
        if is_ctx:
            kv_jobs = [(1, N, "ctx")]
        else:
            kv_jobs = [(129, N, "main")]
            if i == 0:
                kv_jobs.append((1, 128, "blk0"))

        def ktarget(kind, m):
            if kind == "ctx":
                return kcbuf[:, m, 0:N], ("kc", m)
            if kind == "main":
                return kbuf[kvpar][:, m, 0:N], ("k", kvpar, m)
            return k0buf[:, m, :], ("k0", m)

        rope_pend = []

        def rope_finish():
            pz, pzc, n, tb, tgt, tcell = rope_pend.pop(0)
            zb, zbc = p16.next()
            OP("act", "activation", zb[:, 0:n], pz[:, 0:n], AF.Copy, reads=[pzc], writes=[zbc])
            pw, pwc = psr.next()
            while pwc == pzc:
                pw, pwc = psr.next()
            MM(pw[:, 0:n], permm[:, :], zb[:, 0:n], True, True, reads=[zbc, ("permm",)], writes=[pwc])
            t1, t1c = p32.next()
            t2, t2c = p32.next()
            OP("dve", "tensor_tensor", t1[:, 0:n], pz[:, 0:n], cs_t[:, tb:tb + n], ALU.mult,
               reads=[pzc, zbc, ("cs",)], writes=[t1c])
            OP("dve", "tensor_tensor", t2[:, 0:n], pw[:, 0:n], sn_t[:, tb:tb + n], ALU.mult,
               reads=[pwc, ("sn",)], writes=[t2c])
            OP("pool", "tensor_tensor", tgt, t1[:, 0:n], t2[:, 0:n], ALU.add,
               reads=[t1c, t2c], writes=[tcell])

        def rope_push(item):
            rope_pend.append(item)
            if len(rope_pend) > 1:
                rope_finish()

        def rope_flush():
            while rope_pend:
                rope_finish()

        for m in range(2):
            wv, wc = tape.get(L, ("k", m))
            for (c0, n, kind) in kv_jobs:
                pb, pc = psr.next()
                for kc in range(8):
                    MM(pb[:, 0:n], wv[:, kc, :], hT[:, kc, c0:c0 + n], kc == 0, kc == 7,
                       reads=[wc, ("hT", hk, kc)], writes=[pc])
                tgt, tcell = ktarget(kind, m)
                if is_ctx:
                    OP("act", "activation", tgt, pb[:, 0:n], AF.Copy, reads=[pc], writes=[tcell])
                else:
                    rope_push((pb, pc, n, c0 - 1, tgt, tcell))
                npump_holder[0](1)
        rope_flush()
        wv, wc = tape.get(L, ("v",))
        for (c0, n, kind) in kv_jobs:
            for jb in range(n // 128):
                pb, pc = psr.next()
                for kc in range(8):
                    MM(pb[:, 0:256], hT[:, kc, c0 + 128 * jb:c0 + 128 * jb + 128], wv[:, kc, :], kc == 0, kc == 7,
                       reads=[wc, ("hT", hk, kc)], writes=[pc])
                if kind == "ctx":
                    vt, vcell = vcbuf[:, jb, :, :], ("vc",)
                elif kind == "main":
                    vt, vcell = vbuf[kvpar][:, jb, :, :], ("v", kvpar)
                else:
                    vt, vcell = v0buf[:, :, :], ("v0",)
                pv = pb[:, 0:256].rearrange("p (g d) -> p g d", g=4)
                OP("act", "activation", vt[:, 0:4:2, 0:64], pv[:, 0:4:2, :], AF.Copy, reads=[pc], writes=[vcell])
                OP("act", "activation", vt[:, 1:4:2, 64:128], pv[:, 1:4:2, :], AF.Copy, reads=[pc], writes=[vcell])

        for m in range(2):
            for jj in range(4):
                pz, pzc, _, _ = proj_fm(L, ("q", m, jj), 1, N, hT, hk)
                qt = qbuf[:, m * 4 + jj, 0:N]
                qcell = ("q", m * 4 + jj)
                if is_ctx:
                    OP("act", "activation", qt, pz[:, 0:N], AF.Copy, reads=[pzc], writes=[qcell])
                else:
                    rope_push((pz, pzc, N, 0, qt, qcell))
                npump_holder[0](1)
        rope_flush()


        def kv_loc(B):
            if B == 0:
                return (lambda m, b0: k0buf[b0:b0 + 64, m, :], lambda m: ("k0", m),
                        lambda g: v0buf[:, g, :], ("v0",))
            tl = (B - 1) // 4
            sl = (B - 1) % 4
            pp = tl % 2
            return (lambda m, b0: kbuf[pp][b0:b0 + 64, m, sl * 128:(sl + 1) * 128], lambda m: ("k", pp, m),
                    lambda g: vbuf[pp][:, sl, g, :], ("v", pp))

        def ctx_loc(cb):
            return (lambda m, b0: kcbuf[b0:b0 + 64, m, cb * 128:(cb + 1) * 128], lambda m: ("kc", m),
                    lambda g: vcbuf[:, cb, g, :], ("vc",))

        def attention_gen():
            jobs = []
            for j in range(nblk):
                chunks = []
                if not is_ctx:
                    B = 4 * i + j
                    if B >= 1:
                        chunks.append((kv_loc(B - 1), "P"))
                    chunks.append((kv_loc(B), None))
                    chunks.append((kv_loc(B + 1), "N"))
                chunks.append((ctx_loc(0), None))
                chunks.append((ctx_loc(1), None))
                for g in range(4):
                    unit = {"j": j, "g": g}
                    for ci, (loc, msk) in enumerate(chunks):
                        jobs.append({"u": unit, "ci": ci, "n": len(chunks), "loc": loc, "msk": msk})

            def qk(job):
                j, g = job["u"]["j"], job["u"]["g"]
                m = g // 2
                b0 = (g % 2) * 64
                kf, kcf, vf, vcell = job["loc"]
                st, stc = ps_att_st.next()
                qcells = [("q", m * 4 + jj) for jj in range(4)]
                MM(st[:, 0:512], kf(m, b0), qbuf[b0:b0 + 64, m * 4:m * 4 + 4, j * 128:(j + 1) * 128], True, True,
                   reads=[kcf(m)] + qcells, writes=[stc])
                pt, ptc = p16.next()
                OP("act", "activation", pt[:, :], st[:, 0:512], AF.Exp, scale=0.125, reads=[stc], writes=[ptc])
                if job["msk"] is not None:
                    mk = (maskp if job["msk"] == "P" else maskn)
                    OP("pool", "tensor_tensor", pt[:, :].rearrange("p (h q) -> p h q", h=4),
                       pt[:, :].rearrange("p (h q) -> p h q", h=4),
                       mk[:, :].unsqueeze(1).broadcast_to([128, 4, 128]), ALU.mult,
                       reads=[ptc, ("maskp",), ("maskn",)], writes=[ptc])
                job["pt"] = (pt, ptc)

            def pv(job):
                u = job["u"]
                j, g = u["j"], u["g"]
                m = g // 2
                kf, kcf, vf, vcell = job["loc"]
                if job["ci"] == 0:
                    u["ot"] = ps_att_ot.next()
                ot, otc = u["ot"]
                pt, ptc = job["pt"]
                MM(ot[:, 0:512], vf(g), pt[:, :], job["ci"] == 0, False, reads=[vcell, ptc], writes=[otc])
                if job["ci"] == job["n"] - 1:
                    MM(ot[:, 0:512], sinkl[0:1, g % 2, :], esink[0:1, g * 512:(g + 1) * 512], False, True,
                       reads=[("sinkl",), ("esink", L)], writes=[otc])
                    nb_, db_ = (0, 64) if g % 2 == 0 else (64, 0)
                    rd, rdc = p32.next()
                    OP("dve", "reciprocal", rd[nb_:nb_ + 64, 0:512], ot[db_:db_ + 64, 0:512], reads=[otc], writes=[rdc])
                    ycells = [("Y", m * 4 + jj, j, g % 2) for jj in range(4)]
                    OP("dve", "tensor_tensor", ybig[nb_:nb_ + 64, m * 4:m * 4 + 4, j * 128:(j + 1) * 128],
                       ot[nb_:nb_ + 64, 0:512].rearrange("p (h q) -> p h q", h=4),
                       rd[nb_:nb_ + 64, 0:512].rearrange("p (h q) -> p h q", h=4), ALU.mult,
                       reads=[otc, rdc], writes=ycells)

            LOOK = 2
            for k in range(len(jobs) + LOOK):
                if k < len(jobs):
                    qk(jobs[k])
                if k >= LOOK:
                    pv(jobs[k - LOOK])
                yield

        att = attention_gen()
        att_state = {"done": False}

        def pump(n):
            for _ in range(n):
                if att_state["done"]:
                    return
                try:
                    next(att)
                except StopIteration:
                    att_state["done"] = True

        psr.restrict([4, 5, 6, 7])

        first = is_ctx or i == 0
        zero_right = is_ctx
        for cc in range(8):
            wcg, wcgc = tape.get(L, ("cg", cc), hold_prev=True)
            whb, whbc = tape.get(L, ("hb", cc), hold_prev=True)
            hbs, hbsc = p32.next()
            phb, phbc = psr.next()
            for kc in range(8):
                MM(phb[:, 0:N], whb[:, kc, :], hT[:, kc, 2:2 + N], kc == 0, kc == 7,
                   reads=[whbc, ("hT", hk, kc)], writes=[phbc])
            OP("act", "activation", hbs[:, 2:N + 2], phb[:, 0:N], AF.Copy, reads=[phbc], writes=[hbsc])
            if first:
                phb1, phb1c = psr.next()
                for kc in range(8):
                    MM(phb1[:, 0:1], whb[:, kc, :], hT[:, kc, 1:2], kc == 0, kc == 7,
                       reads=[whbc, ("hT", hk, kc)], writes=[phb1c])
                OP("act", "activation", hbs[:, 1:2], phb1[:, 0:1], AF.Copy, reads=[phb1c], writes=[hbsc])
            mt, mtc = p32.next()
            pcg, pcgc = psr.next()
            for kc in range(8):
                MM(pcg[:, 0:N], wcg[:, kc, :], hT[:, kc, 2:2 + N], kc == 0, kc == 7,
                   reads=[wcgc, ("hT", hk, kc)], writes=[pcgc])
            OP("dve", "tensor_tensor", mt[:, 2:N + 2], pcg[:, 0:N], hbs[:, 2:N + 2], ALU.mult,
               reads=[pcgc, hbsc], writes=[mtc])
            if first:
                pcg1, pcg1c = psr.next()
                for kc in range(8):
                    MM(pcg1[:, 0:1], wcg[:, kc, :], hT[:, kc, 1:2], kc == 0, kc == 7,
                       reads=[wcgc, ("hT", hk, kc)], writes=[pcg1c])
                OP("dve", "tensor_tensor", mt[:, 1:2], pcg1[:, 0:1], hbs[:, 1:2], ALU.mult,
                   reads=[pcg1c, hbsc], writes=[mtc])
                OP("pool", "memset", mt[:, 0:1], 0.0, writes=[mtc])
            else:
                OP("pool", "tensor_copy", mt[:, 0:2], mcarry[:, cc, :], reads=[("mcarry", cc)], writes=[mtc])
            if zero_right:
                OP("pool", "memset", mt[:, N + 1:N + 2], 0.0, writes=[mtc])
            if not is_ctx:
                OP("pool", "tensor_copy", mcarry[:, cc, :], mt[:, N:N + 2], reads=[mtc], writes=[("mcarry", cc)])
            pump(6)
            cv, cvc = p32.next()
            OP("act", "activation", cv[:, 0:N], mt[:, 1:N + 1], AF.Copy, scale=wsc[:, L, cc * 3 + 1:cc * 3 + 2],
               reads=[mtc, ("wsc", L)], writes=[cvc])
            OP("dve", "scalar_tensor_tensor", cv[:, 0:N], mt[:, 0:N], wsc[:, L, cc * 3:cc * 3 + 1], cv[:, 0:N],
               ALU.mult, ALU.add, reads=[mtc, cvc, ("wsc", L)], writes=[cvc])
            OP("dve", "scalar_tensor_tensor", cv[:, 0:N], mt[:, 2:N + 2], wsc[:, L, cc * 3 + 2:cc * 3 + 3], cv[:, 0:N],
               ALU.mult, ALU.add, reads=[mtc, cvc, ("wsc", L)], writes=[cvc])
            pbg, pbgc, _, _ = proj_fm(L, ("bg", cc), 1, N, hT, hk)
            OP("dve", "tensor_tensor", ybig[:, 16 + cc, 0:N], pbg[:, 0:N], cv[:, 0:N], ALU.mult,
               reads=[pbgc, cvc], writes=Y(16 + cc))
            pump(5)
        pump(10000)
        psr.restrict([2, 3, 4, 5, 6, 7])
        yield "hook"

        for gg in range(8):
            pb, pc, _, _ = proj_fm(L, ("au", gg), 1, N, hT, hk)
            OP("act", "activation", uT[:, gg, 0:N], pb[:, 0:N], AF.Gelu_apprx_tanh, reads=[pc], writes=[("uT", gg)])
            npump_holder[0](1)
        av = [tape.get(L, ("av", hh), hold_prev=True) for hh in range(2)]

        def a_proj(j):
            gvt, gvc = gvr.next()
            for hh in range(2):
                wv, wc = av[hh]
                pb, pc = psr.next()
                for kc in range(8):
                    MM(pb[:, 0:512], hT[:, kc, 1 + 128 * j:1 + 128 * (j + 1)], wv[:, kc, :],
                       kc == 0, kc == 7, reads=[wc, ("hT", hk, kc)], writes=[pc])
                OP("act", "activation", gvt[:, hh * 512:(hh + 1) * 512], pb[:, 0:512], AF.Gelu_apprx_tanh,
                   reads=[pc], writes=[gvc])
            ssq, ssqc = small_next()
            vnt, vnc = vnr.next()
            OP("act", "activation", vnt[:, :], gvt[:, :], AF.Square, accum_out=ssq, reads=[gvc],
               writes=[vnc, ssqc])
            rt, rtc = small_next()
            OP("pool", "tensor_scalar", rt, ssq, 1.0 / 1024, EPS, ALU.mult, ALU.add, reads=[ssqc], writes=[rtc])
            rv, rvc = small_next()
            OP("pool", "tensor_tensor", rv, rt, mhalf[:, 0:1], ALU.pow, reads=[rtc, ("mhalf",)], writes=[rvc])
            OP("dve", "scalar_tensor_tensor", vnt[:, :], gvt[:, :], rv, gvb[:, :], ALU.mult, ALU.mult,
               reads=[gvc, rvc, ("gvb", L)], writes=[vnc])
            return vnt, vnc

        def a_mix(j, vnt, vnc):
            for bk in range(2):
                mp, mpc = psr.next()
                for g4 in range(4):
                    gg = bk * 4 + g4
                    MM(mp[:, g4 * 128:(g4 + 1) * 128], vnt[:, gg * 128:(gg + 1) * 128], wsT[:, gg * 128:(gg + 1) * 128],
                       True, False, reads=[vnc, ("wsT",)], writes=[mpc])
                    MM(mp[:, g4 * 128:(g4 + 1) * 128], ones_bf[0:1, :], bsrow[0:1, gg * 128:(gg + 1) * 128],
                       False, True, reads=[("ones",), ("bsrow",)], writes=[mpc])
                OP("dve", "tensor_tensor", ybig[:, 8 + bk * 4:8 + bk * 4 + 4, j * 128:(j + 1) * 128],
                   mp[:, 0:512].rearrange("p (g q) -> p g q", g=4),
                   uT[:, bk * 4:bk * 4 + 4, j * 128:(j + 1) * 128], ALU.mult,
                   reads=[mpc] + [("uT", bk * 4 + x) for x in range(4)],
                   writes=[c_ for x in range(4) for c_ in Y(8 + bk * 4 + x, (j,))])

        a_pend = None
        for j in range(nblk):
            cur = a_proj(j)
            npump_holder[0](1)
            if a_pend is not None:
                a_mix(a_pend[0], a_pend[1], a_pend[2])
                npump_holder[0](1)
            a_pend = (j, cur[0], cur[1])
        a_mix(a_pend[0], a_pend[1], a_pend[2])
        npump_holder[0](1)

        def ycell_list(i3, kc):
            return Y(i3 * 8 + kc)

        for oc in range(8):
            npump_holder[0](1)
            acc, accc = p32.next()
            for i3 in range(3):
                wv, wc = tape.get(L, ("br", i3, oc))
                pb, pc = psr.next()
                for kc in range(8):
                    MM(pb[:, 0:N], wv[:, kc, :], ybig[:, i3 * 8 + kc, 0:N], kc == 0, kc == 7,
                       reads=[wc] + ycell_list(i3, kc), writes=[pc])
                pg, pgc, _, _ = proj_fm(L, ("gt", i3, oc), 1, N, hT, hk)
                gt_, gtc = p32.next()
                OP("act", "activation", gt_[:, 0:N], pg[:, 0:N], AF.Sigmoid,
                   bias=bgate[:, L, i3 * 8 + oc:i3 * 8 + oc + 1], scale=1.0,
                   reads=[pgc, ("bgate", L)], writes=[gtc])
                if i3 == 0:
                    OP("dve", "tensor_tensor", acc[:, 0:N], pb[:, 0:N], gt_[:, 0:N], ALU.mult,
                       reads=[pc, gtc], writes=[accc])
                else:
                    tm, tmc = p32.next()
                    OP("dve", "tensor_tensor", tm[:, 0:N], pb[:, 0:N], gt_[:, 0:N], ALU.mult,
                       reads=[pc, gtc], writes=[tmc])
                    if i3 == 1:
                        OP("pool", "tensor_tensor", acc[:, 0:N], acc[:, 0:N], tm[:, 0:N], ALU.add,
                           reads=[accc, tmc], writes=[accc])
                    else:
                        OP("pool", "tensor_tensor", qbuf[:, oc, 0:N], acc[:, 0:N], tm[:, 0:N], ALU.add,
                           reads=[accc, tmc], writes=[("q", oc)])
        for oc in range(8):
            wv, wc = tape.get(L, ("wo", oc))
            pb, pc = psr.next()
            for kc in range(8):
                MM(pb[:, 0:N], wv[:, kc, :], qbuf[:, kc, 0:N], kc == 0, kc == 7,
                   reads=[wc, ("q", kc)], writes=[pc])
            OP("dve", "scalar_tensor_tensor", xt[par][:, oc, 1:N + 1], pb[:, 0:N], modvec(L, "gt1", oc, v),
               xt[par][:, oc, 1:N + 1], ALU.mult, ALU.add, reads=[pc, ("xt", par, oc)] + mcs, writes=[("xt", par, oc)])
            if dst is not None:
                DMA("sp", ("xs", par, oc % 4), dst[oc, :, s:s + N], xt[par][:, oc, 1:N + 1],
                    reads=[("xt", par, oc)], writes=[(dst_name, i, oc)])
            if dbg_dst is not None:
                DMA("sp", ("xs", par, oc % 4), dbg_dst[oc, :, s:s + N], xt[par][:, oc, 1:N + 1],
                    reads=[("xt", par, oc)], writes=[])
        npump_holder[0](1000)
        psr.restrict(None)

    def ctx_kv_tile(L, src, src_name):
        N = NCTX
        par = cur_par[0]
        hT, hk = hTs[par], par
        mcs = modcells(L)
        OP("pool", "memset", xt[par][:, :, 0:1], 0.0, writes=XT(par))
        yield from load_x(par, 1, src, src_name, 0, N)
        yield "loads"
        for _ in norm(L, par, N + 1, lambda kc: modvec(L, "a1", kc, 1), lambda kc: modvec(L, "sh1", kc, 1), 1, mcs, hT, hk):
            yield "norm_step"
        yield "head"
        tape.new_tile()
        psr.restrict([2, 3, 4, 5, 6, 7])
        yield "hook"
        for m in range(2):
            wv, wc = tape.get(L, ("k", m))
            pb, pc = psr.next()
            for kc in range(8):
                MM(pb[:, 0:N], wv[:, kc, :], hT[:, kc, 1:1 + N], kc == 0, kc == 7, reads=[wc, ("hT", hk, kc)], writes=[pc])
            OP("act", "activation", kcbuf[:, m, 0:N], pb[:, 0:N], AF.Copy, reads=[pc], writes=[("kc", m)])
        wv, wc = tape.get(L, ("v",))
        for jb in range(2):
            pb, pc = psr.next()
            for kc in range(8):
                MM(pb[:, 0:256], hT[:, kc, 1 + 128 * jb:1 + 128 * jb + 128], wv[:, kc, :], kc == 0, kc == 7,
                   reads=[wc, ("hT", hk, kc)], writes=[pc])
            vt = vcbuf[:, jb, :, :]
            pv = pb[:, 0:256].rearrange("p (g d) -> p g d", g=4)
            OP("act", "activation", vt[:, 0:4:2, 0:64], pv[:, 0:4:2, :], AF.Copy, reads=[pc], writes=[("vc",)])
            OP("act", "activation", vt[:, 1:4:2, 64:128], pv[:, 1:4:2, :], AF.Copy, reads=[pc], writes=[("vc",)])
        npump_holder[0](1000)
        psr.restrict(None)

    def ffn_tile(L, i, nblk, src, src_name, dst, dst_name, is_ctx, final, dbg_dst=None):
        N = 128 * nblk
        s = TILE * i
        v = 1 if is_ctx else 0
        par = cur_par[0]
        hT, hk = hTs[par], par
        mcs = modcells(L)
        ncol = N + 2
        if is_ctx:
            OP("pool", "memset", xt[par][:, :, 0:1], 0.0, writes=XT(par))
            OP("pool", "memset", xt[par][:, :, N + 1:N + 2], 0.0, writes=XT(par))
            yield from load_x(par, 1, src, src_name, 0, N)
        elif i == 0:
            OP("pool", "memset", xt[par][:, :, 0:1], 0.0, writes=XT(par))
            yield from load_x(par, 1, src, src_name, 0, ncol - 1)
        else:
            yield from load_x(par, 0, src, src_name, s - 1, ncol)
        yield "loads"
        for _ in norm(L, par, ncol, lambda kc: modvec(L, "a2", kc, v), lambda kc: modvec(L, "sh2", kc, v), v, mcs, hT, hk):
            yield "norm_step"
        yield "head"
        tape.new_tile()
        first = is_ctx or i == 0
        zero_right = is_ctx
        pend = None
        psr.restrict([2, 3, 4, 5, 6, 7])
        for jj in range(NJ + 1):
            if jj == 6:
                yield "hook"
            if jj != 6:
                npump_holder[0](1)
            if jj < NJ:
                pa, pac, _, _ = proj_fm(L, ("ua", jj), 2, N, hT, hk, ps_fast)
                at, atc = p32.next()
                OP("act", "activation", at[:, 2:N + 2], pa[:, 0:N], AF.Copy, reads=[pac], writes=[atc])
                if first:
                    pa1, pa1c, _, _ = proj_fm(L, ("ua", jj), 1, 1, hT, hk, ps_fast)
                    OP("act", "activation", at[:, 1:2], pa1[:, 0:1], AF.Copy, reads=[pa1c], writes=[atc])
                    OP("pool", "memset", at[:, 0:1], 0.0, writes=[atc])
                else:
                    OP("pool", "tensor_copy", at[:, 0:2], acarry[:, jj, :], reads=[("acarry", jj)], writes=[atc])
                if zero_right:
                    OP("pool", "memset", at[:, N + 1:N + 2], 0.0, writes=[atc])
                if not is_ctx:
                    OP("pool", "tensor_copy", acarry[:, jj, :], at[:, N:N + 2], reads=[atc], writes=[("acarry", jj)])
                pg, pgc, _, _ = proj_fm(L, ("ug", jj), 1, N, hT, hk, ps_slow)
                cv, cvc = p32.next()
                OP("act", "activation", cv[:, 0:N], at[:, 1:N + 1], AF.Copy, scale=wfc[:, L, jj * 3 + 1:jj * 3 + 2],
                   reads=[atc, ("wfc", L)], writes=[cvc])
                OP("dve", "scalar_tensor_tensor", cv[:, 0:N], at[:, 0:N], wfc[:, L, jj * 3:jj * 3 + 1], cv[:, 0:N],
                   ALU.mult, ALU.add, reads=[atc, cvc, ("wfc", L)], writes=[cvc])
                OP("dve", "scalar_tensor_tensor", cv[:, 0:N], at[:, 2:N + 2], wfc[:, L, jj * 3 + 2:jj * 3 + 3], cv[:, 0:N],
                   ALU.mult, ALU.add, reads=[atc, cvc, ("wfc", L)], writes=[cvc])
            if pend is not None:
                pj, pcv, pcvc, ppg, ppgc = pend
                sl, slc = p32.next()
                OP("act", "activation", sl[:, 0:N], pcv[:, 0:N], AF.Silu, reads=[pcvc], writes=[slc])
                OP("dve", "tensor_tensor", ybig[:, pj, 0:N], ppg[:, 0:N], sl[:, 0:N], ALU.mult,
                   reads=[ppgc, slc], writes=Y(pj))
                pend = None
            if jj < NJ:
                pend = (jj, cv, cvc, pg, pgc)
        for oc in range(8):
            npump_holder[0](1)
            wv, wc = tape.get(L, ("dn", oc))
            pb, pc = psr.next()
            for jj in range(NJ):
                MM(pb[:, 0:N], wv[:, jj, :], ybig[:, jj, 0:N], jj == 0, jj == NJ - 1,
                   reads=[wc] + Y(jj), writes=[pc])
            OP("dve", "scalar_tensor_tensor", xt[par][:, oc, 1:N + 1], pb[:, 0:N], modvec(L, "gt2", oc, v),
               xt[par][:, oc, 1:N + 1], ALU.mult, ALU.add, reads=[pc, ("xt", par, oc)] + mcs, writes=[("xt", par, oc)])
            if not final:
                DMA("sp", ("xs", par, oc % 4), dst[oc, :, s:s + N], xt[par][:, oc, 1:N + 1],
                    reads=[("xt", par, oc)], writes=[(dst_name, i, oc)])
                if dbg_dst is not None:
                    DMA("sp", ("xs", par, oc % 4), dbg_dst[oc, :, s:s + N], xt[par][:, oc, 1:N + 1],
                        reads=[("xt", par, oc)], writes=[])
        if final:
            pb, pc = psr.next()
            for kc in range(8):
                sq, sqc = sqring.next()
                OP("act", "activation", sq[:, 0:N], xt[par][:, kc, 1:N + 1], AF.Square,
                   reads=[("xt", par, kc)], writes=[sqc])
                MM(pb[:, 0:N], ones_bf[:], sq[:, 0:N], kc == 0, kc == 7, reads=[sqc, ("ones",)], writes=[pc])
            rs, rsc = rs_buf, ("rs",)
            OP("act", "activation", rs[:, 0:N], pb[:, 0:N], AF.Sqrt, bias=EPS, scale=1.0 / 1024, reads=[pc], writes=[rsc])
            rstd, rstdc = rstd_buf, ("rstd",)
            OP("dve", "reciprocal", rstd[:, 0:N], rs[:, 0:N], reads=[rsc], writes=[rstdc])
            for kc in range(8):
                OP("dve", "scalar_tensor_tensor", xt[par][:, kc, 1:N + 1], xt[par][:, kc, 1:N + 1], gfin[:, kc:kc + 1],
                   rstd[:, 0:N], ALU.mult, ALU.mult, reads=[("xt", par, kc), rstdc, ("gfin",)], writes=[("xt", par, kc)])
                DMA("sp", ("xs", par, kc % 4), dst[kc, :, s:s + N], xt[par][:, kc, 1:N + 1],
                    reads=[("xt", par, kc)], writes=[(dst_name, i, kc)])
        npump_holder[0](1000)
        psr.restrict(None)

    nseg = len(_SEGS)
    KV_SEGS = _OFFS[("v",)][0] + 1

    def tiles_of(nblocks):
        out = []
        b = 0
        while b < nblocks:
            nb = min(4, nblocks - b)
            out.append((b // 4, nb))
            b += nb
        return out

    def layer_order(nm_tiles, nf_tiles):
        order = []
        for t in range(nm_tiles):
            order.append(("m", t))
            if t >= 1 and t - 1 < nf_tiles:
                order.append(("f", t - 1))
        for t in range(max(nm_tiles - 1, 0), nf_tiles):
            order.append(("f", t))
        return order

    mt0, ft0 = tiles_of(PB[0]), tiles_of(PB[1])
    mt1, ft1 = tiles_of(PB[2]), tiles_of(PB[3])
    conv1 = list(range(nseg))

    def conv_some(n):
        for _ in range(n):
            if conv1:
                convert_seg(1, conv1.pop(0))

    cur_par = [0]
    npump_holder = [lambda n: None]
    def layer_seq(nm, nf):
        seq = []
        fi = 0
        for t in range(nm):
            seq.append(("m", t))
            if t >= 2 and fi < nf:
                seq.append(("f", fi))
                fi += 1
        while fi < nf:
            seq.append(("f", fi))
            fi += 1
        return seq

    seq0 = layer_seq(len(mt0), len(ft0))
    seq1 = layer_seq(len(mt1), len(ft1))
    tiles = []
    tiles.append(("cm", lambda: mixer_tile(0, 0, 2, d_c0, "c0", d_cm1, "cm1", True, True, dbg.get("d_cm1")), 0, "m"))
    k0 = 0
    for kind, t in seq0:
        if kind == "m":
            tiles.append(("m", lambda t=t: mixer_tile(0, mt0[t][0], mt0[t][1], d_x0, "x0", d_xm1, "xm1", False, t == 0,
                                                       dbg.get("d_xm1")), 0, "m"))
            if t == 0:
                tiles.append(("cf", lambda: ffn_tile(0, 0, 2, d_cm1, "cm1", d_c1, "c1", True, False, dbg.get("d_c1")),
                              0, "f"))
        else:
            tiles.append(("f", lambda t=t: ffn_tile(0, ft0[t][0], ft0[t][1], d_xm1, "xm1", d_x1, "x1", False, False,
                                                     dbg.get("d_x1")), 0, "f"))
    tiles.append(("ckv", lambda: ctx_kv_tile(1, d_c1, "c1"), 1, "ckv"))
    for kind, t in seq1:
        if kind == "m":
            tiles.append(("m", lambda t=t: mixer_tile(1, mt1[t][0], mt1[t][1], d_x1, "x1", d_xm2, "xm2", False, t == 0,
                                                       dbg.get("d_xm2")), 1, "m"))
        else:
            tiles.append(("f", lambda t=t: ffn_tile(1, ft1[t][0], ft1[t][1], d_xm2, "xm2", d_out, "out", False, True),
                          1, "f"))
    if upto is not None:
        tiles = tiles[:upto]
    for nm, mk, Lx, wk in tiles:
        if wk == "m":
            tape.extend(Lx, 0, _SEG_FFN0)
        elif wk == "f":
            tape.extend(Lx, _SEG_FFN0, nseg)
        else:
            tape.extend(Lx, 0, KV_SEGS)
    gens = [mk() for (nm, mk, Lx, wk) in tiles]
    bufkind = [k % 2 for k in range(len(tiles))]
    layer_consts_done = {0: True}

    lweave = {"g": None}
    weave = {"g": None}

    def start_loads(k):
        Ln = tiles[k][2]
        if Ln not in layer_consts_done:
            layer_consts_done[Ln] = True
            conv_some(len(conv1))
            load_layer_consts(Ln)
        cur_par[0] = bufkind[k]
        lweave["g"] = gens[k]

    def lpump(n):
        g = lweave["g"]
        if g is None:
            return
        for _ in range(n):
            r = next(g)
            if r == "loads":
                lweave["g"] = None
                return
            assert r == "load_step", r

    def drain_loads():
        while lweave["g"] is not None:
            lpump(1)

    def npump(n):
        lpump(n)
        g = weave["g"]
        if g is None:
            return
        for _ in range(n):
            r = next(g)
            if r == "head":
                weave["g"] = None
                return
            assert r == "norm_step", r

    npump_holder[0] = npump

    def drain_norm():
        while weave["g"] is not None:
            npump(1)

    if gens:
        start_loads(0)
        drain_loads()
        weave["g"] = gens[0]
        drain_norm()
    for k, g in enumerate(gens):
        Lx = tiles[k][2]
        if k + 1 < len(gens):
            start_loads(k + 1)
        r = next(g)
        assert r == "hook", r
        if k + 1 < len(gens):
            drain_loads()
            weave["g"] = gens[k + 1]
        for r in g:
            pass
        drain_norm()
        if Lx == 0:
            conv_some(3)

    S.emit(nc)
    build_program.sbuf_left = nc.sbuf_bytes_remaining
    es.close()
    return nc


def _prep_core_inputs(b, h, x, c, ctx, c_ctx, shared):
    if h == 0:
        pos = np.arange(T0)
        cpos = np.arange(NCTX)
    else:
        pos = SEQ - 1 - np.arange(T0)
        cpos = NCTX - 1 - np.arange(NCTX)
    xT = np.ascontiguousarray(x[b][pos].T).reshape(8, 128, T0)
    ctxT = np.ascontiguousarray(ctx[b][cpos].T).reshape(8, 128, NCTX)
    cvec = np.stack([_vecT(c[b], 8), _vecT(c_ctx, 8)], axis=-1)
    cosT, sinT = _rope_tables(pos)
    d = dict(shared["common"])
    d.update(shared["mirror"][h])
    d.update({"xT": xT, "ctxT": ctxT, "cvec": np.ascontiguousarray(cvec), "cosT": cosT, "sinT": sinT})
    return d


def _prep_shared(w_mod, b_mod, g_mix, w_in, b_gate, sink, w_spatial, b_spatial, g_v, w_sconv, w_branch, w_out,
                 g_ffn, w_up, w_fconv, w_down, g_final):
    f = np.float32
    tape = np.stack([_build_tape(w_in[L], w_branch[L], w_out[L], w_up[L], w_down[L]) for L in range(2)])
    jj, pp = np.meshgrid(np.arange(128), np.arange(128), indexing="ij")
    common = {
        "w_mod": np.ascontiguousarray(w_mod, dtype=f),
        "bmodT": np.stack([_vecT(b_mod[L], 48) for L in range(2)]),
        "gmixT": np.stack([_vecT(g_mix[L], 8) for L in range(2)]),
        "gffnT": np.stack([_vecT(g_ffn[L], 8) for L in range(2)]),
        "gfinT": _vecT(g_final, 8),
        "bgateT": np.stack([_vecT(b_gate[L], 24) for L in range(2)]),
        "sinkx": np.ascontiguousarray(sink.astype(f).reshape(1, 32)),
        "gvb": np.ascontiguousarray(np.broadcast_to(g_v.astype(f)[:, None, :], (2, 128, 1024))),
        "maskP": (jj >= pp).astype(ml_dtypes.bfloat16),
        "maskN": (jj <= pp).astype(ml_dtypes.bfloat16),
        "permM": _perm_matrix(),
        "tape32": tape,
    }
    mirror = []
    for h in range(2):
        ws = w_spatial.astype(f)
        bs = b_spatial.astype(f)
        sc = w_sconv.astype(f)
        fc = w_fconv.astype(f)
        if h == 1:
            ws = ws[:, :, ::-1, ::-1]
            bs = bs[:, :, ::-1]
            sc = sc[:, ::-1, :]
            fc = fc[:, ::-1, :]
        wsT = np.ascontiguousarray(ws.transpose(0, 3, 1, 2)).reshape(2, 128, 1024)
        bsrow = np.ascontiguousarray(bs).reshape(2, 1, 1024)
        wsconvT = np.ascontiguousarray(sc.reshape(2, 3, 8, 128).transpose(0, 3, 2, 1)).reshape(2, 128, 24)
        wfconvT = np.ascontiguousarray(fc.reshape(2, 3, NJ, 128).transpose(0, 3, 2, 1)).reshape(2, 128, 66)
        mirror.append({"wsT": wsT, "bsrow": bsrow, "wsconvT": wsconvT, "wfconvT": wfconvT})
    return {"common": common, "mirror": mirror}


_NC_CACHE = {}


def kernel(x, c, ctx, c_ctx, w_mod, b_mod, g_mix, w_in, b_gate, sink, w_spatial, b_spatial, g_v, w_sconv,
           w_branch, w_out, g_ffn, w_up, w_fconv, w_down, g_final, _debug=False, _upto=None, _ncores=8):
    args = [np.asarray(a, dtype=np.float32) for a in (x, c, ctx, c_ctx, w_mod, b_mod, g_mix, w_in, b_gate, sink,
                                                        w_spatial, b_spatial, g_v, w_sconv, w_branch, w_out, g_ffn,
                                                        w_up, w_fconv, w_down, g_final)]
    (x, c, ctx, c_ctx, w_mod, b_mod, g_mix, w_in, b_gate, sink, w_spatial, b_spatial, g_v, w_sconv, w_branch,
     w_out, g_ffn, w_up, w_fconv, w_down, g_final) = args
    shared = _prep_shared(w_mod, b_mod, g_mix, w_in, b_gate, sink, w_spatial, b_spatial, g_v, w_sconv, w_branch,
                          w_out, g_ffn, w_up, w_fconv, w_down, g_final)
    in_maps = []
    for core in range(_ncores):
        b, h = core // 2, core % 2
        in_maps.append(_prep_core_inputs(b, h, x, c, ctx, c_ctx, shared))
    key = (bool(_debug), _upto)
    if key not in _NC_CACHE:
        _NC_CACHE[key] = build_program(debug=_debug, upto=_upto)
    nc = _NC_CACHE[key]
    res = run_bass_kernel_spmd(nc, in_maps, core_ids=list(range(_ncores)))
    out = np.zeros((4, SEQ, D), np.float32)
    for core in range(_ncores):
        b, h = core // 2, core % 2
        oT = np.asarray(res.results[core]["outT"]).reshape(1024, HALF)
        if h == 0:
            out[b, 0:HALF, :] = oT.T
        else:
            out[b, SEQ - 1 - np.arange(HALF), :] = oT.T
    if _debug:
        return out, res
    return out
```

```python
import numpy as np
import ml_dtypes
import concourse.bass as bass
import concourse.mybir as mybir
from concourse.bass_utils import run_bass_kernel_spmd

F32 = mybir.dt.float32
BF16 = mybir.dt.bfloat16
AF = mybir.ActivationFunctionType
ALU = mybir.AluOpType

D = 1024
KC = 8
SEQ = 8192
HALF = 4096
NCTX = 256
DFF = 2816
NJ = 22
OFF_Q, OFF_K, OFF_V, OFF_A, OFF_B, OFF_G = 0, 1024, 1280, 1536, 3584, 6656
IN_W = 9728
EPS = 1e-6
TILE = 512
_DBG_LABEL = [None]
_DBG_MM = []
PB = [35, 34, 33, 32]
T0 = 36 * 128
SLOT = 4096
NSLOT = 4
PREFETCH = 2


def _slab(wmat, rows, cols):
    sub = wmat[np.asarray(rows)[:, None], np.asarray(cols)[None, :]]
    kt, wd = sub.shape
    return np.ascontiguousarray(sub.reshape(kt // 128, 128, wd).transpose(1, 0, 2)).reshape(128, -1)


def _tape_plan():
    slabs = []
    for m in range(2):
        slabs.append((("k", m), 128, 8))
    slabs.append((("v",), 256, 8))
    for m in range(2):
        for jj in range(4):
            slabs.append((("q", m, jj), 128, 8))
    for cc in range(8):
        slabs.append((("cg", cc), 128, 8))
        slabs.append((("hb", cc), 128, 8))
        slabs.append((("bg", cc), 128, 8))
    for gg in range(8):
        slabs.append((("au", gg), 128, 8))
    for hh in range(2):
        slabs.append((("av", hh), 512, 8))
    for oc in range(8):
        for i in range(3):
            slabs.append((("br", i, oc), 128, 8))
            slabs.append((("gt", i, oc), 128, 8))
    for oc in range(8):
        slabs.append((("wo", oc), 128, 8))
    n_mixer = len(slabs)
    for jj in range(NJ):
        slabs.append((("ua", jj), 128, 8))
        slabs.append((("ug", jj), 128, 8))
    for oc in range(8):
        slabs.append((("dn", oc), 128, NJ))
    offs = {}
    segs = []
    cur = None
    off = 0
    for idx, (nm, wd, kcn) in enumerate(slabs):
        ln = wd * kcn
        if cur is None or cur[1] + ln > SLOT or idx == n_mixer:
            cur = [off, 0, []]
            segs.append(cur)
        offs[nm] = (len(segs) - 1, cur[1], wd, kcn)
        cur[1] += ln
        cur[2].append(nm)
        off += ln
    seg_mixer = offs[("ua", 0)][0]
    return slabs, offs, segs, off, seg_mixer


_SLABS, _OFFS, _SEGS, _LT, _SEG_FFN0 = _tape_plan()


def _head_cols(base, head):
    return list(range(base + head * 64, base + head * 64 + 64))


def _swap64(cols):
    return [cols[(d + 32) % 64] for d in range(64)]


def _build_tape(w_in, w_branch, w_out, w_up, w_down):
    tape = np.empty((128, _LT), np.float32)
    allr = np.arange(1024)
    off = 0
    for (nm, wd, kcn) in _SLABS:
        kind = nm[0]
        if kind in ("k", "ks"):
            m = nm[1]
            c0 = _head_cols(OFF_K, 2 * m)
            c1 = _head_cols(OFF_K, 2 * m + 1)
            if kind == "ks":
                c0, c1 = _swap64(c0), _swap64(c1)
            s = _slab(w_in, allr, c0 + c1)
        elif kind == "v":
            s = _slab(w_in, allr, range(OFF_V, OFF_V + 256))
        elif kind in ("q", "qs"):
            m, jj = nm[1], nm[2]
            c0 = _head_cols(OFF_Q, 4 * (2 * m) + jj)
            c1 = _head_cols(OFF_Q, 4 * (2 * m + 1) + jj)
            if kind == "qs":
                c0, c1 = _swap64(c0), _swap64(c1)
            s = _slab(w_in, allr, c0 + c1)
        elif kind == "au":
            s = _slab(w_in, allr, range(OFF_A + nm[1] * 128, OFF_A + nm[1] * 128 + 128))
        elif kind == "av":
            s = _slab(w_in, allr, range(OFF_A + 1024 + nm[1] * 512, OFF_A + 1024 + nm[1] * 512 + 512))
        elif kind == "bg":
            s = _slab(w_in, allr, range(OFF_B + nm[1] * 128, OFF_B + nm[1] * 128 + 128))
        elif kind == "cg":
            s = _slab(w_in, allr, range(OFF_B + 1024 + nm[1] * 128, OFF_B + 1024 + nm[1] * 128 + 128))
        elif kind == "hb":
            s = _slab(w_in, allr, range(OFF_B + 2048 + nm[1] * 128, OFF_B + 2048 + nm[1] * 128 + 128))
        elif kind == "gt":
            i, oc = nm[1], nm[2]
            s = _slab(w_in, allr, range(OFF_G + i * 1024 + oc * 128, OFF_G + i * 1024 + oc * 128 + 128))
        elif kind == "br":
            i, oc = nm[1], nm[2]
            if i == 0:
                rows = []
                for m in range(2):
                    for jj in range(4):
                        rows += _head_cols(0, 4 * (2 * m) + jj) + _head_cols(0, 4 * (2 * m + 1) + jj)
            else:
                rows = allr
            s = _slab(w_branch[i], rows, range(oc * 128, oc * 128 + 128))
        elif kind == "wo":
            s = _slab(w_out, allr, range(nm[1] * 128, nm[1] * 128 + 128))
        elif kind == "ua":
            s = _slab(w_up, allr, range(nm[1] * 128, nm[1] * 128 + 128))
        elif kind == "ug":
            s = _slab(w_up, allr, range(DFF + nm[1] * 128, DFF + nm[1] * 128 + 128))
        elif kind == "dn":
            s = _slab(w_down, np.arange(DFF), range(nm[1] * 128, nm[1] * 128 + 128))
        else:
            raise AssertionError(nm)
        ln = wd * kcn
        assert s.shape[1] == ln
        tape[:, off:off + ln] = s
        off += ln
    return tape


def _perm_matrix():
    pm = np.zeros((128, 128), np.float32)
    for mcol in range(128):
        pm[64 * (mcol // 64) + (mcol % 64 + 32) % 64, mcol] = 1.0
    return pm.astype(ml_dtypes.bfloat16)


def _vecT(v, n):
    return np.ascontiguousarray(np.asarray(v, np.float32).reshape(n, 128).T)


def _rope_tables(pos):
    pos = np.asarray(pos)
    row = (pos // 64).astype(np.float32)
    col = (pos % 64).astype(np.float32)
    inv = (np.float32(10000.0) ** (-np.arange(0, 32, 2, dtype=np.float32) / np.float32(32))).astype(np.float32)
    ang = np.concatenate([row[:, None] * inv, col[:, None] * inv], axis=-1).astype(np.float32)
    cos = np.cos(ang).astype(np.float32).T
    sin = np.sin(ang).astype(np.float32).T
    cosT = np.tile(cos, (4, 1))
    sinT = np.tile(np.concatenate([-sin, sin], axis=0), (2, 1))
    return np.ascontiguousarray(cosT), np.ascontiguousarray(sinT)


class _Op:
    __slots__ = ("eng", "fn", "deps", "sig", "stream", "ordn", "val", "vc", "dma")


class _Cell:
    __slots__ = ("writer", "readers")

    def __init__(self):
        self.writer = None
        self.readers = {}


class Sched:
    ENGS = ("pe", "act", "dve", "pool", "sp")

    def __init__(self):
        self.ops = []
        self.cells = {}
        self.stream_count = {}
        self.last_dma = {}

    def add(self, eng, fn, reads=(), writes=(), dma=None):
        op = _Op()
        op.eng = eng
        op.fn = fn
        op.dma = dma
        op.stream = ("dma", dma) if dma is not None else ("eng", eng)
        op.sig = dma is not None
        op.val = 0
        op.vc = None
        n = self.stream_count.get(op.stream, 0) + 1
        self.stream_count[op.stream] = n
        op.ordn = n
        deps = {}

        def dep(o):
            if o is None or o is op:
                return
            if eng == "pe" and o.eng == "pe" and o.dma is None and dma is None:
                return
            cur = deps.get(o.stream)
            if cur is None or cur.ordn < o.ordn:
                deps[o.stream] = o

        cells = self.cells
        for r in reads:
            c = cells.get(r)
            if c is None:
                c = cells[r] = _Cell()
            dep(c.writer)
        for w in writes:
            c = cells.get(w)
            if c is None:
                c = cells[w] = _Cell()
            dep(c.writer)
            for o in c.readers.values():
                dep(o)
        if dma is not None:
            dep(self.last_dma.get(dma))
            self.last_dma[dma] = op
        for r in reads:
            cells[r].readers[op.stream] = op
        for w in writes:
            c = cells[w]
            c.writer = op
            c.readers = {}
        for o in deps.values():
            o.sig = True
        op.deps = list(deps.values())
        self.ops.append(op)
        return op

    def emit(self, nc):
        streams = sorted(self.stream_count.keys(), key=str)
        sidx = {s: i for i, s in enumerate(streams)}
        ns = len(streams)
        import contextlib
        stack = contextlib.ExitStack()
        sems = {}
        for s in streams:
            sems[s] = stack.enter_context(nc.semaphore("s_" + "_".join(str(x) for x in s)))
        know = {e: np.zeros(ns, np.int64) for e in self.ENGS}
        counts = {s: 0 for s in streams}
        prog = {e: [] for e in self.ENGS}
        for op in self.ops:
            k = know[op.eng]
            lst = prog[op.eng]
            for d in op.deps:
                si = sidx[d.stream]
                if k[si] < d.val:
                    lst.append((0, sems[d.stream], int(d.val)))
                    np.maximum(k, d.vc, out=k)
            if op.sig:
                inc = 16 if op.dma is not None else 1
                counts[op.stream] += inc
                op.val = counts[op.stream]
                vc = k.copy()
                vc[sidx[op.stream]] = op.val
                op.vc = vc
                lst.append((1, op.fn, sems[op.stream], inc))
            else:
                lst.append((1, op.fn, None, 0))
        for s in streams:
            if s[0] == "dma" and counts[s] > 0:
                prog["sp"].append((0, sems[s], counts[s]))
        self.ops = None
        self.cells = None

        def run(engh, lst):
            for it in lst:
                if it[0] == 0:
                    engh.wait_ge(it[1], it[2])
                else:
                    ins = it[1](engh)
                    if it[2] is not None:
                        ins.then_inc(it[2], it[3])

        with stack:
            with nc.Block() as block:
                @block.tensor
                def _(e):
                    run(e, prog["pe"])

                @block.scalar
                def _(e):
                    run(e, prog["act"])

                @block.vector
                def _(e):
                    run(e, prog["dve"])

                @block.gpsimd
                def _(e):
                    run(e, prog["pool"])

                @block.sync
                def _(e):
                    run(e, prog["sp"])


class Ring:
    def __init__(self, name, bufs):
        self.name = name
        self.bufs = bufs
        self.i = -1

    def next(self):
        self.i = (self.i + 1) % len(self.bufs)
        return self.bufs[self.i], (self.name, self.i)


def build_program(debug=False, upto=None):
    import contextlib
    nc = bass.Bass("TRN2", target_bir_lowering=False)
    S = Sched()
    es = contextlib.ExitStack()

    def dram(name, shape, dt, kind):
        return nc.dram_tensor(name, list(shape), dt, kind=kind).ap()

    def sb(name, shape, dt):
        return es.enter_context(nc.sbuf_tensor(name, list(shape), dt))

    d_x0 = dram("xT", [8, 128, T0], F32, "ExternalInput")
    d_c0 = dram("ctxT", [8, 128, NCTX], F32, "ExternalInput")
    d_cvec = dram("cvec", [128, 8, 2], F32, "ExternalInput")
    d_wmod = dram("w_mod", [2, 1024, 6144], F32, "ExternalInput")
    d_bmod = dram("bmodT", [2, 128, 48], F32, "ExternalInput")
    d_gmix = dram("gmixT", [2, 128, 8], F32, "ExternalInput")
    d_gffn = dram("gffnT", [2, 128, 8], F32, "ExternalInput")
    d_gfin = dram("gfinT", [128, 8], F32, "ExternalInput")
    d_bgate = dram("bgateT", [2, 128, 24], F32, "ExternalInput")
    d_sink = dram("sinkx", [1, 32], F32, "ExternalInput")
    d_ws = dram("wsT", [2, 128, 1024], F32, "ExternalInput")
    d_bs = dram("bsrow", [2, 1, 1024], F32, "ExternalInput")
    d_gvb = dram("gvb", [2, 128, 1024], F32, "ExternalInput")
    d_wsc = dram("wsconvT", [2, 128, 24], F32, "ExternalInput")
    d_wfc = dram("wfconvT", [2, 128, 66], F32, "ExternalInput")
    d_cos = dram("cosT", [128, T0], F32, "ExternalInput")
    d_sin = dram("sinT", [128, T0], F32, "ExternalInput")
    d_maskp = dram("maskP", [128, 128], BF16, "ExternalInput")
    d_maskn = dram("maskN", [128, 128], BF16, "ExternalInput")
    d_perm = dram("permM", [128, 128], BF16, "ExternalInput")
    d_tape = dram("tape32", [2, 128, _LT], F32, "ExternalInput")
    d_out = dram("outT", [8, 128, HALF], F32, "ExternalOutput")
    d_t16 = dram("tape16", [2, 128, _LT], BF16, "Internal")
    d_xm1 = dram("xm1", [8, 128, PB[0] * 128], F32, "Internal")
    d_x1 = dram("x1", [8, 128, PB[1] * 128], F32, "Internal")
    d_xm2 = dram("xm2", [8, 128, PB[2] * 128], F32, "Internal")
    d_cm1 = dram("cm1", [8, 128, NCTX], F32, "Internal")
    d_c1 = dram("c1", [8, 128, NCTX], F32, "Internal")
    dbg = {}
    if debug:
        for nm, shp in (("d_xm1", [8, 128, PB[0] * 128]), ("d_x1", [8, 128, PB[1] * 128]),
                        ("d_xm2", [8, 128, PB[2] * 128]), ("d_cm1", [8, 128, NCTX]), ("d_c1", [8, 128, NCTX])):
            dbg[nm] = dram(nm, shp, F32, "ExternalOutput")

    def pk(ap):
        return ap.rearrange("k p t -> p k t")

    wring = [sb(f"wr{i}", [128, SLOT], BF16) for i in range(NSLOT)]
    xt = [sb("xt_a", [128, 8, 641], F32), sb("xt_b", [128, 8, 641], F32)]
    hTs = [sb("hT_a", [128, 8, 642], BF16), sb("hT_b", [128, 8, 642], BF16)]
    sqring = Ring("sq", [sb(f"sq{i}", [128, 641], BF16) for i in range(2)])
    n32 = Ring("n32", [sb(f"n32_{i}", [128, 641], F32) for i in range(2)])
    rs_buf = sb("rs_buf", [128, 641], F32)
    rstd_buf = sb("rstd_buf", [128, 641], F32)
    p32 = Ring("p32", [sb(f"p32_{i}", [128, 514], F32) for i in range(7)])
    p16 = Ring("p16", [sb(f"p16_{i}", [128, 512], BF16) for i in range(4)])
    cs_t = sb("cs_t", [128, 640], F32)
    sn_t = sb("sn_t", [128, 640], F32)
    qbuf = sb("qbuf", [128, 8, 512], BF16)
    kbuf = [sb(f"kbuf{i}", [128, 2, 512], BF16) for i in range(2)]
    k0buf = sb("k0buf", [128, 2, 128], BF16)
    vbuf = [sb(f"vbuf{i}", [128, 4, 4, 128], BF16) for i in range(2)]
    v0buf = sb("v0buf", [128, 4, 128], BF16)
    kcbuf = sb("kcbuf", [128, 2, 256], BF16)
    vcbuf = sb("vcbuf", [128, 2, 4, 128], BF16)
    ybig = sb("ybig", [128, 24, 512], BF16)
    uT = sb("uT", [128, 8, 512], BF16)
    gvr = Ring("gv", [sb(f"gv{i}", [128, 1024], BF16) for i in range(2)])
    vnr = Ring("vn", [sb(f"vn{i}", [128, 1024], BF16) for i in range(2)])
    mcarry = sb("mcarry", [128, 8, 2], F32)
    acarry = sb("acarry", [128, NJ, 2], F32)
    small = sb("small", [128, 16], F32)
    mhalf = sb("mhalf", [128, 1], F32)
    ones_bf = sb("ones_bf", [128, 128], BF16)
    sinkl = sb("sinkl", [1, 2, 128], BF16)
    maskp = sb("maskp", [128, 128], BF16)
    maskn = sb("maskn", [128, 128], BF16)
    permm = sb("permm", [128, 128], BF16)
    esink = sb("esink", [1, 2048], BF16)
    sink32 = sb("sink32", [1, 32], F32)
    wsT = sb("wsTb", [128, 1024], BF16)
    bsrow = sb("bsrowb", [1, 1024], BF16)
    gvb = sb("gvbs", [128, 1024], BF16)
    wsc = sb("wscs", [128, 2, 24], F32)
    wfc = sb("wfcs", [128, 2, 66], F32)
    bgate = sb("bgates", [128, 2, 24], F32)
    gmix = sb("gmixs", [128, 2, 8], F32)
    gffn = sb("gffns", [128, 2, 8], F32)
    gfin = sb("gfins", [128, 8], F32)
    bmod = sb("bmods", [128, 2, 48], F32)
    cvec = sb("cvecs", [128, 8, 2], F32)
    scv = sb("scv", [128, 8, 2], F32)
    modv = sb("modv", [128, 2, 48, 2], F32)
    a1v = sb("a1v", [128, 2, 8, 2], F32)
    a2v = sb("a2v", [128, 2, 8, 2], F32)
    gth = sb("gth", [128, 2, 8, 2], F32)
    bgh = sb("bgh", [128, 2, 24], F32)

    ps_pairs = [es.enter_context(nc.psum_tensor(f"ps{i}", [128, 1024], F32)) for i in range(4)]
    ps_banks = [ps_pairs[i // 2][:, (i % 2) * 512:(i % 2) * 512 + 512] for i in range(8)]

    class PsRing:
        def __init__(self, banks):
            self.all = list(banks)
            self.banks = list(banks)
            self.k = -1

        def restrict(self, banks):
            self.banks = list(banks) if banks is not None else list(self.all)
            self.k = -1

        def _check(self, b):
            c = S.cells.get(("ps", b))
            assert c is None or c.writer is None or len(c.readers) > 0, f"psum bank {b} re-allocated while open"

        def next(self):
            self.k = (self.k + 1) % len(self.banks)
            b = self.banks[self.k]
            self._check(b)
            return ps_banks[b], ("ps", b)

        def next2(self):
            for _ in range(len(self.banks)):
                self.k = (self.k + 1) % len(self.banks)
                b = self.banks[self.k]
                if b % 2 == 0 and (b + 1) in self.banks:
                    break
            else:
                raise AssertionError("no pair")
            self._check(b)
            self._check(b + 1)
            self.k = self.banks.index(b + 1)
            return ps_pairs[b // 2], [("ps", b), ("ps", b + 1)]

    psr = PsRing(range(8))
    ps_att_st = PsRing([0, 1])
    psr_norm = PsRing([0, 1])
    ps_fast = PsRing([2, 3])
    ps_slow = PsRing([4, 5, 6, 7])
    ps_att_ot = PsRing([2, 3])

    def XT(par):
        return [("xt", par, kc) for kc in range(8)]

    def OP(eng, name, *args, reads=(), writes=(), **kw):
        def fn(e, name=name, args=args, kw=kw):
            return getattr(e, name)(*args, **kw)
        return S.add(eng, fn, reads, writes)

    def DMA(eng, slot, out, in_, reads=(), writes=()):
        def fn(e, out=out, in_=in_):
            return e.dma_start(out=out, in_=in_)
        return S.add(eng, fn, reads, writes, dma=slot)

    def MM(ps, lhsT, rhs, start, stop, reads, writes):
        _DBG_MM.append((_DBG_LABEL[0], int(np.prod(rhs.shape[1:])), bool(start), str(writes[0][0]), str(reads[0][0])))
        def fn(e, ps=ps, lhsT=lhsT, rhs=rhs, start=start, stop=stop):
            return e.matmul(ps, lhsT=lhsT, rhs=rhs, start=start, stop=stop)
        return S.add("pe", fn, reads, writes)

    class Tape:
        def __init__(self):
            self.plan = []
            self.issued = 0
            self.pos = -1
            self.force = False

        def new_tile(self):
            self.force = True

        def extend(self, L, seg_lo, seg_hi):
            for sg in range(seg_lo, seg_hi):
                self.plan.append((L, sg))

        def _issue(self, k):
            L, sg = self.plan[k]
            off, ln, _ = _SEGS[sg]
            slot = k % NSLOT
            DMA("sp", ("w", slot), wring[slot][:, 0:ln], d_t16[L, :, off:off + ln],
                reads=[("t16", L, sg)], writes=[("w", slot)])

        def get(self, L, name, hold_prev=False):
            _DBG_LABEL[0] = (L, name)
            sg, soff, wd, kcn = _OFFS[name]
            if self.pos < 0 or self.force or self.plan[self.pos] != (L, sg):
                self.force = False
                self.pos += 1
                assert self.plan[self.pos] == (L, sg), (self.plan[self.pos], L, sg, name)
            while self.issued < min(len(self.plan), self.pos + (PREFETCH if hold_prev else PREFETCH + 1) + 1):
                self._issue(self.issued)
                self.issued += 1
            slot = self.pos % NSLOT
            view = wring[slot][:, soff:soff + wd * kcn].rearrange("p (k c) -> p k c", k=kcn)
            return view, ("w", slot)

    tape = Tape()

    conv_state = {"n": 0}

    def convert_seg(L, sg):
        off, ln, _ = _SEGS[sg]
        n = conv_state["n"]
        conv_state["n"] += 1
        DMA("pool", ("cv", n % 8), d_t16[L, :, off:off + ln], d_tape[L, :, off:off + ln],
            reads=[], writes=[("t16", L, sg)])

    ld = {"n": 0}

    def LOAD(out, in_, cell):
        n = ld["n"]
        ld["n"] += 1
        DMA("sp", ("ld", n), out, in_, reads=[], writes=[cell])

    LOAD(maskp[:], d_maskp, ("maskp",))
    LOAD(maskn[:], d_maskn, ("maskn",))
    LOAD(permm[:], d_perm, ("permm",))
    LOAD(cvec[:], d_cvec, ("cvec",))
    LOAD(gfin[:], d_gfin, ("gfin",))
    LOAD(sink32[:], d_sink, ("sink32",))
    for L in range(2):
        LOAD(bmod[:, L, :], d_bmod[L], ("bmod", L))
        LOAD(gmix[:, L, :], d_gmix[L], ("gmix", L))
        LOAD(gffn[:, L, :], d_gffn[L], ("gffn", L))
        LOAD(bgate[:, L, :], d_bgate[L], ("bgate", L))
        LOAD(wsc[:, L, :], d_wsc[L], ("wsc", L))
        LOAD(wfc[:, L, :], d_wfc[L], ("wfc", L))
    OP("pool", "memset", ones_bf[:], 1.0, writes=[("ones",)])
    OP("pool", "memset", mhalf[:], -0.5, writes=[("mhalf",)])
    OP("pool", "memset", sinkl[:, 0, 0:64], 0.0, writes=[("sinkl",)])
    OP("pool", "memset", sinkl[:, 0, 64:128], 1.0, writes=[("sinkl",)])
    OP("pool", "memset", sinkl[:, 1, 0:64], 1.0, writes=[("sinkl",)])
    OP("pool", "memset", sinkl[:, 1, 64:128], 0.0, writes=[("sinkl",)])
    for i in range(2):
        OP("pool", "memset", vbuf[i][:], 1.0, writes=[("v", i)])
    OP("pool", "memset", v0buf[:], 1.0, writes=[("v0",)])
    OP("pool", "memset", vcbuf[:], 1.0, writes=[("vc",)])

    def load_layer_consts(L):
        DMA("pool", ("lc", 0), wsT[:], d_ws[L], reads=[], writes=[("wsT",)])
        DMA("pool", ("lc", 1), bsrow[:], d_bs[L], reads=[], writes=[("bsrow",)])
        DMA("pool", ("lc", 2), gvb[:], d_gvb[L], reads=[], writes=[("gvb", L)])
        OP("act", "activation", esink[0:1, :].rearrange("p (h q) -> p h q", h=16),
           sink32[0:1, L * 16:(L + 1) * 16].unsqueeze(2).broadcast_to([1, 16, 128]), AF.Exp,
           reads=[("sink32",)], writes=[("esink", L)])

    load_layer_consts(0)
    for sg in range(len(_SEGS)):
        convert_seg(0, sg)

    OP("act", "activation", scv[:], cvec[:], AF.Silu, reads=[("cvec",)], writes=[("scv",)])
    for L in range(2):
        mps, mcell = psr.next()
        for pc in range(12):
            par = (pc + 1) % 2 if False else pc % 2
            stage = xt[par][:].rearrange("p k t -> p (k t)")[:, 0:4096].rearrange("p (k c) -> p k c", k=8)
            src = d_wmod[L].rearrange("(k p) c -> p k c", p=128)[:, :, pc * 512:(pc + 1) * 512]
            DMA("sp", ("xlp", par), stage, src, reads=[], writes=XT(par))
            for c4 in range(4):
                ch = pc * 4 + c4
                for kc in range(8):
                    MM(mps[:, ch * 2:ch * 2 + 2], stage[:, kc, c4 * 128:(c4 + 1) * 128], scv[:, kc, :],
                       kc == 0, kc == 7, reads=XT(par) + [("scv",)], writes=[mcell])
        bm = bmod[:, L, :].unsqueeze(2).broadcast_to([128, 48, 2])
        OP("dve", "tensor_tensor", modv[:, L, :, :], mps[:, 0:96].rearrange("p (c v) -> p c v", v=2), bm,
           ALU.add, reads=[mcell, ("bmod", L)], writes=[("modv", L)])
        OP("dve", "scalar_tensor_tensor", a1v[:, L, :, :], modv[:, L, 8:16, :], 1.0,
           gmix[:, L, :].unsqueeze(2).broadcast_to([128, 8, 2]), ALU.add, ALU.mult,
           reads=[("modv", L), ("gmix", L)], writes=[("a1v", L)])
        OP("dve", "scalar_tensor_tensor", a2v[:, L, :, :], modv[:, L, 32:40, :], 1.0,
           gffn[:, L, :].unsqueeze(2).broadcast_to([128, 8, 2]), ALU.add, ALU.mult,
           reads=[("modv", L), ("gffn", L)], writes=[("a2v", L)])
        OP("dve", "tensor_scalar", gth[:, L, :, :], modv[:, L, 16:24, :], 0.5, None, ALU.mult,
           reads=[("modv", L)], writes=[("gth", L)])
        OP("dve", "tensor_scalar", bgh[:, L, :], bgate[:, L, :], 0.5, None, ALU.mult,
           reads=[("bgate", L)], writes=[("bgh", L)])

    def modvec(L, which, kc, v):
        if which == "a1":
            return a1v[:, L, kc, v:v + 1]
        if which == "a2":
            return a2v[:, L, kc, v:v + 1]
        base = {"sh1": 0, "gt1": 16, "sh2": 24, "gt2": 40}[which]
        return modv[:, L, base + kc, v:v + 1]

    def Y(chunk, js=(0, 1, 2, 3), hs=(0, 1)):
        return [("Y", chunk, j, h) for j in js for h in hs]

    def modcells(L):
        return [("modv", L), ("a1v", L), ("a2v", L), ("gth", L)]

    small_i = {"n": 0}

    def small_next():
        small_i["n"] = (small_i["n"] + 1) % 16
        i = small_i["n"]
        return small[:, i:i + 1], ("small", i)

    def norm(L, par, ncol, avec, shvec, v, mcs, hT, hk):
        groups = [(0, min(512, ncol))]
        if ncol > 512:
            groups.append((512, ncol - 512))
        pss = [psr_norm.next() for _ in groups]
        for kc in range(8):
            sq, sqc = sqring.next()
            OP("act", "activation", sq[:, 0:ncol], xt[par][:, kc, 0:ncol], AF.Square,
               reads=[("xt", par, kc)], writes=[sqc])
            yield "norm_step"
            for (c0, w), (pb, pc) in zip(groups, pss):
                MM(pb[:, 0:w], ones_bf[:], sq[:, c0:c0 + w], kc == 0, kc == 7,
                   reads=[sqc, ("ones",)], writes=[pc])
        rs, rsc = rs_buf, ("rs",)
        for (c0, w), (pb, pc) in zip(groups, pss):
            OP("act", "activation", rs[:, c0:c0 + w], pb[:, 0:w], AF.Sqrt, bias=EPS, scale=1.0 / 1024,
               reads=[pc], writes=[rsc])
        rstd, rstdc = rstd_buf, ("rstd",)
        OP("dve", "reciprocal", rstd[:, 0:ncol], rs[:, 0:ncol], reads=[rsc], writes=[rstdc])
        yield "norm_step"
        for kc in range(8):
            tmp, tc_ = n32.next()
            OP("dve", "scalar_tensor_tensor", tmp[:, 0:ncol], xt[par][:, kc, 0:ncol], avec(kc), rstd[:, 0:ncol],
               ALU.mult, ALU.mult, reads=[("xt", par, kc), rstdc] + mcs, writes=[tc_])
            OP("act", "activation", hT[:, kc, 0:ncol], tmp[:, 0:ncol], AF.Identity, bias=shvec(kc), scale=1.0,
               reads=[tc_] + mcs, writes=[("hT", hk, kc)])
            yield "norm_step"

    def proj_fm(L, name, c0, n, hT, hk, ring=None):
        wv, wc = tape.get(L, name)
        kcn = wv.shape[1]
        pb, pc = (ring or psr).next()
        for kc in range(kcn):
            MM(pb[:, 0:n], wv[:, kc, :], hT[:, kc, c0:c0 + n], kc == 0, kc == kcn - 1,
               reads=[wc, ("hT", hk, kc)], writes=[pc])
        return pb, pc, wv, wc

    def load_x(par, dst_c0, src, src_name, t0, w):
        tl = list(range(t0 // TILE, (t0 + w - 1) // TILE + 1))
        for kc in range(8):
            DMA("pool", ("xl", par, kc % 4), xt[par][:, kc, dst_c0:dst_c0 + w], src[kc, :, t0:t0 + w],
                reads=[(src_name, t, kc) for t in tl], writes=[("xt", par, kc)])
            yield "load_step"

    def mixer_tile(L, i, nblk, src, src_name, dst, dst_name, is_ctx, first_of_stage, dbg_dst=None):
        N = 128 * nblk
        s = TILE * i
        v = 1 if is_ctx else 0
        par = cur_par[0]
        hT, hk = hTs[par], par
        mcs = modcells(L)
        if is_ctx:
            ncol = N + 2
            OP("pool", "memset", xt[par][:, :, 0:1], 0.0, writes=XT(par))
            OP("pool", "memset", xt[par][:, :, N + 1:N + 2], 0.0, writes=XT(par))
            yield from load_x(par, 1, src, src_name, 0, N)
        else:
            ncol = N + 129
            if i == 0:
                OP("pool", "memset", xt[par][:, :, 0:1], 0.0, writes=XT(par))
                yield from load_x(par, 1, src, src_name, 0, ncol - 1)
            else:
                yield from load_x(par, 0, src, src_name, s - 1, ncol)
        yield "loads"
        if not is_ctx:
            DMA("pool", ("rope", 0), cs_t[:, 0:N + 128], d_cos[:, s:s + N + 128], reads=[], writes=[("cs",)])
            DMA("pool", ("rope", 1), sn_t[:, 0:N + 128], d_sin[:, s:s + N + 128], reads=[], writes=[("sn",)])
        for _ in norm(L, par, ncol, lambda kc: modvec(L, "a1", kc, v), lambda kc: modvec(L, "sh1", kc, v), v, mcs, hT, hk):
            yield "norm_step"
        yield "head"
        tape.new_tile()

        kvpar = i % 2
        if is_ctx:
            kv_jobs = [(1, N, "ctx")]
        else:
            kv_jobs = [(129, N, "main")]
            if i == 0:
                kv_jobs.append((1, 128, "blk0"))

        def ktarget(kind, m):
            if kind == "ctx":
                return kcbuf[:, m, 0:N], ("kc", m)
            if kind == "main":
                return kbuf[kvpar][:, m, 0:N], ("k", kvpar, m)
            return k0buf[:, m, :], ("k0", m)

        rope_pend = []

        def rope_finish():
            pz, pzc, n, tb, tgt, tcell = rope_pend.pop(0)
            zb, zbc = p16.next()
            OP("act", "activation", zb[:, 0:n], pz[:, 0:n], AF.Copy, reads=[pzc], writes=[zbc])
            pw, pwc = psr.next()
            while pwc == pzc:
                pw, pwc = psr.next()
            MM(pw[:, 0:n], permm[:, :], zb[:, 0:n], True, True, reads=[zbc, ("permm",)], writes=[pwc])
            t1, t1c = p32.next()
            t2, t2c = p32.next()
            OP("dve", "tensor_tensor", t1[:, 0:n], pz[:, 0:n], cs_t[:, tb:tb + n], ALU.mult,
               reads=[pzc, zbc, ("cs",)], writes=[t1c])
            OP("dve", "tensor_tensor", t2[:, 0:n], pw[:, 0:n], sn_t[:, tb:tb + n], ALU.mult,
               reads=[pwc, ("sn",)], writes=[t2c])
            OP("pool", "tensor_tensor", tgt, t1[:, 0:n], t2[:, 0:n], ALU.add,
               reads=[t1c, t2c], writes=[tcell])

        def rope_push(item):
            rope_pend.append(item)
            if len(rope_pend) > 1:
                rope_finish()

        def rope_flush():
            while rope_pend:
                rope_finish()

        for m in range(2):
            wv, wc = tape.get(L, ("k", m))
            for (c0, n, kind) in kv_jobs:
                pb, pc = psr.next()
                for kc in range(8):
                    MM(pb[:, 0:n], wv[:, kc, :], hT[:, kc, c0:c0 + n], kc == 0, kc == 7,
                       reads=[wc, ("hT", hk, kc)], writes=[pc])
                tgt, tcell = ktarget(kind, m)
                if is_ctx:
                    OP("act", "activation", tgt, pb[:, 0:n], AF.Copy, reads=[pc], writes=[tcell])
                else:
                    rope_push((pb, pc, n, c0 - 1, tgt, tcell))
                npump_holder[0](1)
        rope_flush()
        wv, wc = tape.get(L, ("v",))
        for (c0, n, kind) in kv_jobs:
            for jb in range(n // 128):
                pb, pc = psr.next()
                for kc in range(8):
                    MM(pb[:, 0:256], hT[:, kc, c0 + 128 * jb:c0 + 128 * jb + 128], wv[:, kc, :], kc == 0, kc == 7,
                       reads=[wc, ("hT", hk, kc)], writes=[pc])
                if kind == "ctx":
                    vt, vcell = vcbuf[:, jb, :, :], ("vc",)
                elif kind == "main":
                    vt, vcell = vbuf[kvpar][:, jb, :, :], ("v", kvpar)
                else:
                    vt, vcell = v0buf[:, :, :], ("v0",)
                pv = pb[:, 0:256].rearrange("p (g d) -> p g d", g=4)
                OP("act", "activation", vt[:, 0:4:2, 0:64], pv[:, 0:4:2, :], AF.Copy, reads=[pc], writes=[vcell])
                OP("act", "activation", vt[:, 1:4:2, 64:128], pv[:, 1:4:2, :], AF.Copy, reads=[pc], writes=[vcell])

        for m in range(2):
            for jj in range(4):
                pz, pzc, _, _ = proj_fm(L, ("q", m, jj), 1, N, hT, hk)
                qt = qbuf[:, m * 4 + jj, 0:N]
                qcell = ("q", m * 4 + jj)
                if is_ctx:
                    OP("act", "activation", qt, pz[:, 0:N], AF.Copy, reads=[pzc], writes=[qcell])
                else:
                    rope_push((pz, pzc, N, 0, qt, qcell))
                npump_holder[0](1)
        rope_flush()


        def kv_loc(B):
            if B == 0:
                return (lambda m, b0: k0buf[b0:b0 + 64, m, :], lambda m: ("k0", m),
                        lambda g: v0buf[:, g, :], ("v0",))
            tl = (B - 1) // 4
            sl = (B - 1) % 4
            pp = tl % 2
            return (lambda m, b0: kbuf[pp][b0:b0 + 64, m, sl * 128:(sl + 1) * 128], lambda m: ("k", pp, m),
                    lambda g: vbuf[pp][:, sl, g, :], ("v", pp))

        def ctx_loc(cb):
            return (lambda m, b0: kcbuf[b0:b0 + 64, m, cb * 128:(cb + 1) * 128], lambda m: ("kc", m),
                    lambda g: vcbuf[:, cb, g, :], ("vc",))

        def attention_gen():
            jobs = []
            for j in range(nblk):
                chunks = []
                if not is_ctx:
                    B = 4 * i + j
                    if B >= 1:
                        chunks.append((kv_loc(B - 1), "P"))
                    chunks.append((kv_loc(B), None))
                    chunks.append((kv_loc(B + 1), "N"))
                chunks.append((ctx_loc(0), None))
                chunks.append((ctx_loc(1), None))
                for g in range(4):
                    unit = {"j": j, "g": g}
                    for ci, (loc, msk) in enumerate(chunks):
                        jobs.append({"u": unit, "ci": ci, "n": len(chunks), "loc": loc, "msk": msk})

            def qk(job):
                j, g = job["u"]["j"], job["u"]["g"]
                m = g // 2
                b0 = (g % 2) * 64
                kf, kcf, vf, vcell = job["loc"]
                st, stc = ps_att_st.next()
                qcells = [("q", m * 4 + jj) for jj in range(4)]
                MM(st[:, 0:512], kf(m, b0), qbuf[b0:b0 + 64, m * 4:m * 4 + 4, j * 128:(j + 1) * 128], True, True,
                   reads=[kcf(m)] + qcells, writes=[stc])
                pt, ptc = p16.next()
                OP("act", "activation", pt[:, :], st[:, 0:512], AF.Exp, scale=0.125, reads=[stc], writes=[ptc])
                if job["msk"] is not None:
                    mk = (maskp if job["msk"] == "P" else maskn)
                    OP("pool", "tensor_tensor", pt[:, :].rearrange("p (h q) -> p h q", h=4),
                       pt[:, :].rearrange("p (h q) -> p h q", h=4),
                       mk[:, :].unsqueeze(1).broadcast_to([128, 4, 128]), ALU.mult,
                       reads=[ptc, ("maskp",), ("maskn",)], writes=[ptc])
                job["pt"] = (pt, ptc)

            def pv(job):
                u = job["u"]
                j, g = u["j"], u["g"]
                m = g // 2
                kf, kcf, vf, vcell = job["loc"]
                if job["ci"] == 0:
                    u["ot"] = ps_att_ot.next()
                ot, otc = u["ot"]
                pt, ptc = job["pt"]
                MM(ot[:, 0:512], vf(g), pt[:, :], job["ci"] == 0, False, reads=[vcell, ptc], writes=[otc])
                if job["ci"] == job["n"] - 1:
                    MM(ot[:, 0:512], sinkl[0:1, g % 2, :], esink[0:1, g * 512:(g + 1) * 512], False, True,
                       reads=[("sinkl",), ("esink", L)], writes=[otc])
                    nb_, db_ = (0, 64) if g % 2 == 0 else (64, 0)
                    rd, rdc = p32.next()
                    OP("dve", "reciprocal", rd[nb_:nb_ + 64, 0:512], ot[db_:db_ + 64, 0:512], reads=[otc], writes=[rdc])
                    ycells = [("Y", m * 4 + jj, j, g % 2) for jj in range(4)]
                    OP("dve", "tensor_tensor", ybig[nb_:nb_ + 64, m * 4:m * 4 + 4, j * 128:(j + 1) * 128],
                       ot[nb_:nb_ + 64, 0:512].rearrange("p (h q) -> p h q", h=4),
                       rd[nb_:nb_ + 64, 0:512].rearrange("p (h q) -> p h q", h=4), ALU.mult,
                       reads=[otc, rdc], writes=ycells)

            LOOK = 2
            for k in range(len(jobs) + LOOK):
                if k < len(jobs):
                    qk(jobs[k])
                if k >= LOOK:
                    pv(jobs[k - LOOK])
                yield

        att = attention_gen()
        att_state = {"done": False}

        def pump(n):
            for _ in range(n):
                if att_state["done"]:
                    return
                try:
                    next(att)
                except StopIteration:
                    att_state["done"] = True

        psr.restrict([4, 5, 6, 7])

        first = is_ctx or i == 0
        zero_right = is_ctx
        for cc in range(8):
            wcg, wcgc = tape.get(L, ("cg", cc), hold_prev=True)
            whb, whbc = tape.get(L, ("hb", cc), hold_prev=True)
            hbs, hbsc = p32.next()
            phb, phbc = psr.next()
            for kc in range(8):
                MM(phb[:, 0:N], whb[:, kc, :], hT[:, kc, 2:2 + N], kc == 0, kc == 7,
                   reads=[whbc, ("hT", hk, kc)], writes=[phbc])
            OP("act", "activation", hbs[:, 2:N + 2], phb[:, 0:N], AF.Copy, reads=[phbc], writes=[hbsc])
            if first:
                phb1, phb1c = psr.next()
                for kc in range(8):
                    MM(phb1[:, 0:1], whb[:, kc, :], hT[:, kc, 1:2], kc == 0, kc == 7,
                       reads=[whbc, ("hT", hk, kc)], writes=[phb1c])
                OP("act", "activation", hbs[:, 1:2], phb1[:, 0:1], AF.Copy, reads=[phb1c], writes=[hbsc])
            mt, mtc = p32.next()
            pcg, pcgc = psr.next()
            for kc in range(8):
                MM(pcg[:, 0:N], wcg[:, kc, :], hT[:, kc, 2:2 + N], kc == 0, kc == 7,
                   reads=[wcgc, ("hT", hk, kc)], writes=[pcgc])
            OP("dve", "tensor_tensor", mt[:, 2:N + 2], pcg[:, 0:N], hbs[:, 2:N + 2], ALU.mult,
               reads=[pcgc, hbsc], writes=[mtc])
            if first:
                pcg1, pcg1c = psr.next()
                for kc in range(8):
                    MM(pcg1[:, 0:1], wcg[:, kc, :], hT[:, kc, 1:2], kc == 0, kc == 7,
                       reads=[wcgc, ("hT", hk, kc)], writes=[pcg1c])
                OP("dve", "tensor_tensor", mt[:, 1:2], pcg1[:, 0:1], hbs[:, 1:2], ALU.mult,
                   reads=[pcg1c, hbsc], writes=[mtc])
                OP("pool", "memset", mt[:, 0:1], 0.0, writes=[mtc])
            else:
                OP("pool", "tensor_copy", mt[:, 0:2], mcarry[:, cc, :], reads=[("mcarry", cc)], writes=[mtc])
            if zero_right:
                OP("pool", "memset", mt[:, N + 1:N + 2], 0.0, writes=[mtc])
            if not is_ctx:
                OP("pool", "tensor_copy", mcarry[:, cc, :], mt[:, N:N + 2], reads=[mtc], writes=[("mcarry", cc)])
            pump(6)
            cv, cvc = p32.next()
            OP("act", "activation", cv[:, 0:N], mt[:, 1:N + 1], AF.Copy, scale=wsc[:, L, cc * 3 + 1:cc * 3 + 2],
               reads=[mtc, ("wsc", L)], writes=[cvc])
            OP("dve", "scalar_tensor_tensor", cv[:, 0:N], mt[:, 0:N], wsc[:, L, cc * 3:cc * 3 + 1], cv[:, 0:N],
               ALU.mult, ALU.add, reads=[mtc, cvc, ("wsc", L)], writes=[cvc])
            OP("dve", "scalar_tensor_tensor", cv[:, 0:N], mt[:, 2:N + 2], wsc[:, L, cc * 3 + 2:cc * 3 + 3], cv[:, 0:N],
               ALU.mult, ALU.add, reads=[mtc, cvc, ("wsc", L)], writes=[cvc])
            pbg, pbgc, _, _ = proj_fm(L, ("bg", cc), 1, N, hT, hk)
            OP("dve", "tensor_tensor", ybig[:, 16 + cc, 0:N], pbg[:, 0:N], cv[:, 0:N], ALU.mult,
               reads=[pbgc, cvc], writes=Y(16 + cc))
            pump(5)
        pump(10000)
        psr.restrict([2, 3, 4, 5, 6, 7])
        yield "hook"

        for gg in range(8):
            pb, pc, _, _ = proj_fm(L, ("au", gg), 1, N, hT, hk)
            OP("act", "activation", uT[:, gg, 0:N], pb[:, 0:N], AF.Gelu_apprx_tanh, reads=[pc], writes=[("uT", gg)])
            npump_holder[0](1)
        av = [tape.get(L, ("av", hh), hold_prev=True) for hh in range(2)]

        def a_proj(j):
            gvt, gvc = gvr.next()
            for hh in range(2):
                wv, wc = av[hh]
                pb, pc = psr.next()
                for kc in range(8):
                    MM(pb[:, 0:512], hT[:, kc, 1 + 128 * j:1 + 128 * (j + 1)], wv[:, kc, :],
                       kc == 0, kc == 7, reads=[wc, ("hT", hk, kc)], writes=[pc])
                OP("act", "activation", gvt[:, hh * 512:(hh + 1) * 512], pb[:, 0:512], AF.Gelu_apprx_tanh,
                   reads=[pc], writes=[gvc])
            ssq, ssqc = small_next()
            vnt, vnc = vnr.next()
            OP("act", "activation", vnt[:, :], gvt[:, :], AF.Square, accum_out=ssq, reads=[gvc],
               writes=[vnc, ssqc])
            rt, rtc = small_next()
            OP("pool", "tensor_scalar", rt, ssq, 1.0 / 1024, EPS, ALU.mult, ALU.add, reads=[ssqc], writes=[rtc])
            rv, rvc = small_next()
            OP("pool", "tensor_tensor", rv, rt, mhalf[:, 0:1], ALU.pow, reads=[rtc, ("mhalf",)], writes=[rvc])
            OP("dve", "scalar_tensor_tensor", vnt[:, :], gvt[:, :], rv, gvb[:, :], ALU.mult, ALU.mult,
               reads=[gvc, rvc, ("gvb", L)], writes=[vnc])
            return vnt, vnc

        def a_mix(j, vnt, vnc):
            for bk in range(2):
                mp, mpc = psr.next()
                for g4 in range(4):
                    gg = bk * 4 + g4
                    MM(mp[:, g4 * 128:(g4 + 1) * 128], vnt[:, gg * 128:(gg + 1) * 128], wsT[:, gg * 128:(gg + 1) * 128],
                       True, False, reads=[vnc, ("wsT",)], writes=[mpc])
                    MM(mp[:, g4 * 128:(g4 + 1) * 128], ones_bf[0:1, :], bsrow[0:1, gg * 128:(gg + 1) * 128],
                       False, True, reads=[("ones",), ("bsrow",)], writes=[mpc])
                OP("dve", "tensor_tensor", ybig[:, 8 + bk * 4:8 + bk * 4 + 4, j * 128:(j + 1) * 128],
                   mp[:, 0:512].rearrange("p (g q) -> p g q", g=4),
                   uT[:, bk * 4:bk * 4 + 4, j * 128:(j + 1) * 128], ALU.mult,
                   reads=[mpc] + [("uT", bk * 4 + x) for x in range(4)],
                   writes=[c_ for x in range(4) for c_ in Y(8 + bk * 4 + x, (j,))])

        a_pend = None
        for j in range(nblk):
            cur = a_proj(j)
            npump_holder[0](1)
            if a_pend is not None:
                a_mix(a_pend[0], a_pend[1], a_pend[2])
                npump_holder[0](1)
            a_pend = (j, cur[0], cur[1])
        a_mix(a_pend[0], a_pend[1], a_pend[2])
        npump_holder[0](1)

        def ycell_list(i3, kc):
            return Y(i3 * 8 + kc)

        for oc in range(8):
            npump_holder[0](1)
            acc, accc = p32.next()
            for i3 in range(3):
                wv, wc = tape.get(L, ("br", i3, oc))
                pb, pc = psr.next()
                for kc in range(8):
                    MM(pb[:, 0:N], wv[:, kc, :], ybig[:, i3 * 8 + kc, 0:N], kc == 0, kc == 7,
                       reads=[wc] + ycell_list(i3, kc), writes=[pc])
                pg, pgc, _, _ = proj_fm(L, ("gt", i3, oc), 1, N, hT, hk)
                gt_, gtc = p32.next()
                OP("act", "activation", gt_[:, 0:N], pg[:, 0:N], AF.Tanh,
                   bias=bgh[:, L, i3 * 8 + oc:i3 * 8 + oc + 1], scale=0.5,
                   reads=[pgc, ("bgh", L)], writes=[gtc])
                if i3 == 0:
                    OP("dve", "scalar_tensor_tensor", acc[:, 0:N], gt_[:, 0:N], 1.0, pb[:, 0:N], ALU.add, ALU.mult,
                       reads=[pc, gtc], writes=[accc])
                else:
                    tm, tmc = p32.next()
                    OP("dve", "scalar_tensor_tensor", tm[:, 0:N], gt_[:, 0:N], 1.0, pb[:, 0:N], ALU.add, ALU.mult,
                       reads=[pc, gtc], writes=[tmc])
                    if i3 == 1:
                        OP("pool", "tensor_tensor", acc[:, 0:N], acc[:, 0:N], tm[:, 0:N], ALU.add,
                           reads=[accc, tmc], writes=[accc])
                    else:
                        OP("pool", "tensor_tensor", qbuf[:, oc, 0:N], acc[:, 0:N], tm[:, 0:N], ALU.add,
                           reads=[accc, tmc], writes=[("q", oc)])
        for oc in range(8):
            wv, wc = tape.get(L, ("wo", oc))
            pb, pc = psr.next()
            for kc in range(8):
                MM(pb[:, 0:N], wv[:, kc, :], qbuf[:, kc, 0:N], kc == 0, kc == 7,
                   reads=[wc, ("q", kc)], writes=[pc])
            OP("dve", "scalar_tensor_tensor", xt[par][:, oc, 1:N + 1], pb[:, 0:N], gth[:, L, oc, v:v + 1],
               xt[par][:, oc, 1:N + 1], ALU.mult, ALU.add, reads=[pc, ("xt", par, oc)] + mcs, writes=[("xt", par, oc)])
            if dst is not None:
                DMA("sp", ("xs", par, oc % 4), dst[oc, :, s:s + N], xt[par][:, oc, 1:N + 1],
                    reads=[("xt", par, oc)], writes=[(dst_name, i, oc)])
            if dbg_dst is not None:
                DMA("sp", ("xs", par, oc % 4), dbg_dst[oc, :, s:s + N], xt[par][:, oc, 1:N + 1],
                    reads=[("xt", par, oc)], writes=[])
        npump_holder[0](1000)
        psr.restrict(None)

    def ctx_kv_tile(L, src, src_name):
        N = NCTX
        par = cur_par[0]
        hT, hk = hTs[par], par
        mcs = modcells(L)
        OP("pool", "memset", xt[par][:, :, 0:1], 0.0, writes=XT(par))
        yield from load_x(par, 1, src, src_name, 0, N)
        yield "loads"
        for _ in norm(L, par, N + 1, lambda kc: modvec(L, "a1", kc, 1), lambda kc: modvec(L, "sh1", kc, 1), 1, mcs, hT, hk):
            yield "norm_step"
        yield "head"
        tape.new_tile()
        psr.restrict([2, 3, 4, 5, 6, 7])
        yield "hook"
        for m in range(2):
            wv, wc = tape.get(L, ("k", m))
            pb, pc = psr.next()
            for kc in range(8):
                MM(pb[:, 0:N], wv[:, kc, :], hT[:, kc, 1:1 + N], kc == 0, kc == 7, reads=[wc, ("hT", hk, kc)], writes=[pc])
            OP("act", "activation", kcbuf[:, m, 0:N], pb[:, 0:N], AF.Copy, reads=[pc], writes=[("kc", m)])
        wv, wc = tape.get(L, ("v",))
        for jb in range(2):
            pb, pc = psr.next()
            for kc in range(8):
                MM(pb[:, 0:256], hT[:, kc, 1 + 128 * jb:1 + 128 * jb + 128], wv[:, kc, :], kc == 0, kc == 7,
                   reads=[wc, ("hT", hk, kc)], writes=[pc])
            vt = vcbuf[:, jb, :, :]
            pv = pb[:, 0:256].rearrange("p (g d) -> p g d", g=4)
            OP("act", "activation", vt[:, 0:4:2, 0:64], pv[:, 0:4:2, :], AF.Copy, reads=[pc], writes=[("vc",)])
            OP("act", "activation", vt[:, 1:4:2, 64:128], pv[:, 1:4:2, :], AF.Copy, reads=[pc], writes=[("vc",)])
        npump_holder[0](1000)
        psr.restrict(None)

    def ffn_tile(L, i, nblk, src, src_name, dst, dst_name, is_ctx, final, dbg_dst=None):
        N = 128 * nblk
        s = TILE * i
        v = 1 if is_ctx else 0
        par = cur_par[0]
        hT, hk = hTs[par], par
        mcs = modcells(L)
        ncol = N + 2
        if is_ctx:
            OP("pool", "memset", xt[par][:, :, 0:1], 0.0, writes=XT(par))
            OP("pool", "memset", xt[par][:, :, N + 1:N + 2], 0.0, writes=XT(par))
            yield from load_x(par, 1, src, src_name, 0, N)
        elif i == 0:
            OP("pool", "memset", xt[par][:, :, 0:1], 0.0, writes=XT(par))
            yield from load_x(par, 1, src, src_name, 0, ncol - 1)
        else:
            yield from load_x(par, 0, src, src_name, s - 1, ncol)
        yield "loads"
        for _ in norm(L, par, ncol, lambda kc: modvec(L, "a2", kc, v), lambda kc: modvec(L, "sh2", kc, v), v, mcs, hT, hk):
            yield "norm_step"
        yield "head"
        tape.new_tile()
        first = is_ctx or i == 0
        zero_right = is_ctx
        pend = None
        psr.restrict([2, 3, 4, 5, 6, 7])
        for jj in range(NJ + 1):
            if jj == 6:
                yield "hook"
            if jj != 6:
                npump_holder[0](1)
            if jj < NJ:
                pa, pac, _, _ = proj_fm(L, ("ua", jj), 2, N, hT, hk, ps_fast)
                at, atc = p32.next()
                OP("act", "activation", at[:, 2:N + 2], pa[:, 0:N], AF.Copy, reads=[pac], writes=[atc])
                if first:
                    pa1, pa1c, _, _ = proj_fm(L, ("ua", jj), 1, 1, hT, hk, ps_fast)
                    OP("act", "activation", at[:, 1:2], pa1[:, 0:1], AF.Copy, reads=[pa1c], writes=[atc])
                    OP("pool", "memset", at[:, 0:1], 0.0, writes=[atc])
                else:
                    OP("pool", "tensor_copy", at[:, 0:2], acarry[:, jj, :], reads=[("acarry", jj)], writes=[atc])
                if zero_right:
                    OP("pool", "memset", at[:, N + 1:N + 2], 0.0, writes=[atc])
                if not is_ctx:
                    OP("pool", "tensor_copy", acarry[:, jj, :], at[:, N:N + 2], reads=[atc], writes=[("acarry", jj)])
                pg, pgc, _, _ = proj_fm(L, ("ug", jj), 1, N, hT, hk, ps_slow)
                cv, cvc = p32.next()
                OP("act", "activation", cv[:, 0:N], at[:, 1:N + 1], AF.Copy, scale=wfc[:, L, jj * 3 + 1:jj * 3 + 2],
                   reads=[atc, ("wfc", L)], writes=[cvc])
                OP("dve", "scalar_tensor_tensor", cv[:, 0:N], at[:, 0:N], wfc[:, L, jj * 3:jj * 3 + 1], cv[:, 0:N],
                   ALU.mult, ALU.add, reads=[atc, cvc, ("wfc", L)], writes=[cvc])
                OP("dve", "scalar_tensor_tensor", cv[:, 0:N], at[:, 2:N + 2], wfc[:, L, jj * 3 + 2:jj * 3 + 3], cv[:, 0:N],
                   ALU.mult, ALU.add, reads=[atc, cvc, ("wfc", L)], writes=[cvc])
            if pend is not None:
                pj, pcv, pcvc, ppg, ppgc = pend
                sl, slc = p32.next()
                OP("act", "activation", sl[:, 0:N], pcv[:, 0:N], AF.Silu, reads=[pcvc], writes=[slc])
                OP("dve", "tensor_tensor", ybig[:, pj, 0:N], ppg[:, 0:N], sl[:, 0:N], ALU.mult,
                   reads=[ppgc, slc], writes=Y(pj))
                pend = None
            if jj < NJ:
                pend = (jj, cv, cvc, pg, pgc)
        for oc in range(8):
            npump_holder[0](1)
            wv, wc = tape.get(L, ("dn", oc))
            pb, pc = psr.next()
            for jj in range(NJ):
                MM(pb[:, 0:N], wv[:, jj, :], ybig[:, jj, 0:N], jj == 0, jj == NJ - 1,
                   reads=[wc] + Y(jj), writes=[pc])
            OP("dve", "scalar_tensor_tensor", xt[par][:, oc, 1:N + 1], pb[:, 0:N], modvec(L, "gt2", oc, v),
               xt[par][:, oc, 1:N + 1], ALU.mult, ALU.add, reads=[pc, ("xt", par, oc)] + mcs, writes=[("xt", par, oc)])
            if not final:
                DMA("sp", ("xs", par, oc % 4), dst[oc, :, s:s + N], xt[par][:, oc, 1:N + 1],
                    reads=[("xt", par, oc)], writes=[(dst_name, i, oc)])
                if dbg_dst is not None:
                    DMA("sp", ("xs", par, oc % 4), dbg_dst[oc, :, s:s + N], xt[par][:, oc, 1:N + 1],
                        reads=[("xt", par, oc)], writes=[])
        if final:
            pb, pc = psr.next()
            for kc in range(8):
                sq, sqc = sqring.next()
                OP("act", "activation", sq[:, 0:N], xt[par][:, kc, 1:N + 1], AF.Square,
                   reads=[("xt", par, kc)], writes=[sqc])
                MM(pb[:, 0:N], ones_bf[:], sq[:, 0:N], kc == 0, kc == 7, reads=[sqc, ("ones",)], writes=[pc])
            rs, rsc = rs_buf, ("rs",)
            OP("act", "activation", rs[:, 0:N], pb[:, 0:N], AF.Sqrt, bias=EPS, scale=1.0 / 1024, reads=[pc], writes=[rsc])
            rstd, rstdc = rstd_buf, ("rstd",)
            OP("dve", "reciprocal", rstd[:, 0:N], rs[:, 0:N], reads=[rsc], writes=[rstdc])
            for kc in range(8):
                OP("dve", "scalar_tensor_tensor", xt[par][:, kc, 1:N + 1], xt[par][:, kc, 1:N + 1], gfin[:, kc:kc + 1],
                   rstd[:, 0:N], ALU.mult, ALU.mult, reads=[("xt", par, kc), rstdc, ("gfin",)], writes=[("xt", par, kc)])
                DMA("sp", ("xs", par, kc % 4), dst[kc, :, s:s + N], xt[par][:, kc, 1:N + 1],
                    reads=[("xt", par, kc)], writes=[(dst_name, i, kc)])
        npump_holder[0](1000)
        psr.restrict(None)

    nseg = len(_SEGS)
    KV_SEGS = _OFFS[("v",)][0] + 1

    def tiles_of(nblocks):
        out = []
        b = 0
        while b < nblocks:
            nb = min(4, nblocks - b)
            out.append((b // 4, nb))
            b += nb
        return out

    def layer_order(nm_tiles, nf_tiles):
        order = []
        for t in range(nm_tiles):
            order.append(("m", t))
            if t >= 1 and t - 1 < nf_tiles:
                order.append(("f", t - 1))
        for t in range(max(nm_tiles - 1, 0), nf_tiles):
            order.append(("f", t))
        return order

    mt0, ft0 = tiles_of(PB[0]), tiles_of(PB[1])
    mt1, ft1 = tiles_of(PB[2]), tiles_of(PB[3])
    conv1 = list(range(nseg))

    def conv_some(n):
        for _ in range(n):
            if conv1:
                convert_seg(1, conv1.pop(0))

    cur_par = [0]
    npump_holder = [lambda n: None]
    def layer_seq(nm, nf):
        seq = []
        fi = 0
        for t in range(nm):
            seq.append(("m", t))
            if t >= 2 and fi < nf:
                seq.append(("f", fi))
                fi += 1
        while fi < nf:
            seq.append(("f", fi))
            fi += 1
        return seq

    seq0 = layer_seq(len(mt0), len(ft0))
    seq1 = layer_seq(len(mt1), len(ft1))
    tiles = []
    tiles.append(("cm", lambda: mixer_tile(0, 0, 2, d_c0, "c0", d_cm1, "cm1", True, True, dbg.get("d_cm1")), 0, "m"))
    k0 = 0
    for kind, t in seq0:
        if kind == "m":
            tiles.append(("m", lambda t=t: mixer_tile(0, mt0[t][0], mt0[t][1], d_x0, "x0", d_xm1, "xm1", False, t == 0,
                                                       dbg.get("d_xm1")), 0, "m"))
            if t == 0:
                tiles.append(("cf", lambda: ffn_tile(0, 0, 2, d_cm1, "cm1", d_c1, "c1", True, False, dbg.get("d_c1")),
                              0, "f"))
        else:
            tiles.append(("f", lambda t=t: ffn_tile(0, ft0[t][0], ft0[t][1], d_xm1, "xm1", d_x1, "x1", False, False,
                                                     dbg.get("d_x1")), 0, "f"))
    tiles.append(("ckv", lambda: ctx_kv_tile(1, d_c1, "c1"), 1, "ckv"))
    for kind, t in seq1:
        if kind == "m":
            tiles.append(("m", lambda t=t: mixer_tile(1, mt1[t][0], mt1[t][1], d_x1, "x1", d_xm2, "xm2", False, t == 0,
                                                       dbg.get("d_xm2")), 1, "m"))
        else:
            tiles.append(("f", lambda t=t: ffn_tile(1, ft1[t][0], ft1[t][1], d_xm2, "xm2", d_out, "out", False, True),
                          1, "f"))
    if upto is not None:
        tiles = tiles[:upto]
    for nm, mk, Lx, wk in tiles:
        if wk == "m":
            tape.extend(Lx, 0, _SEG_FFN0)
        elif wk == "f":
            tape.extend(Lx, _SEG_FFN0, nseg)
        else:
            tape.extend(Lx, 0, KV_SEGS)
    gens = [mk() for (nm, mk, Lx, wk) in tiles]
    bufkind = [k % 2 for k in range(len(tiles))]
    layer_consts_done = {0: True}

    lweave = {"g": None}
    weave = {"g": None}

    def start_loads(k):
        Ln = tiles[k][2]
        if Ln not in layer_consts_done:
            layer_consts_done[Ln] = True
            conv_some(len(conv1))
            load_layer_consts(Ln)
        cur_par[0] = bufkind[k]
        lweave["g"] = gens[k]

    def lpump(n):
        g = lweave["g"]
        if g is None:
            return
        for _ in range(n):
            r = next(g)
            if r == "loads":
                lweave["g"] = None
                return
            assert r == "load_step", r

    def drain_loads():
        while lweave["g"] is not None:
            lpump(1)

    def npump(n):
        lpump(n)
        g = weave["g"]
        if g is None:
            return
        for _ in range(n):
            r = next(g)
            if r == "head":
                weave["g"] = None
                return
            assert r == "norm_step", r

    npump_holder[0] = npump

    def drain_norm():
        while weave["g"] is not None:
            npump(1)

    if gens:
        start_loads(0)
        drain_loads()
        weave["g"] = gens[0]
        drain_norm()
    for k, g in enumerate(gens):
        Lx = tiles[k][2]
        if k + 1 < len(gens):
            start_loads(k + 1)
        r = next(g)
        assert r == "hook", r
        if k + 1 < len(gens):
            drain_loads()
            weave["g"] = gens[k + 1]
        for r in g:
            pass
        drain_norm()
        if Lx == 0:
            conv_some(3)

    S.emit(nc)
    build_program.sbuf_left = nc.sbuf_bytes_remaining
    es.close()
    return nc


def _prep_core_inputs(b, h, x, c, ctx, c_ctx, shared):
    if h == 0:
        pos = np.arange(T0)
        cpos = np.arange(NCTX)
    else:
        pos = SEQ - 1 - np.arange(T0)
        cpos = NCTX - 1 - np.arange(NCTX)
    xT = np.ascontiguousarray(x[b][pos].T).reshape(8, 128, T0)
    ctxT = np.ascontiguousarray(ctx[b][cpos].T).reshape(8, 128, NCTX)
    cvec = np.stack([_vecT(c[b], 8), _vecT(c_ctx, 8)], axis=-1)
    cosT, sinT = _rope_tables(pos)
    d = dict(shared["common"])
    d.update(shared["mirror"][h])
    d.update({"xT": xT, "ctxT": ctxT, "cvec": np.ascontiguousarray(cvec), "cosT": cosT, "sinT": sinT})
    return d


def _prep_shared(w_mod, b_mod, g_mix, w_in, b_gate, sink, w_spatial, b_spatial, g_v, w_sconv, w_branch, w_out,
                 g_ffn, w_up, w_fconv, w_down, g_final):
    f = np.float32
    tape = np.stack([_build_tape(w_in[L], w_branch[L], w_out[L], w_up[L], w_down[L]) for L in range(2)])
    jj, pp = np.meshgrid(np.arange(128), np.arange(128), indexing="ij")
    common = {
        "w_mod": np.ascontiguousarray(w_mod, dtype=f),
        "bmodT": np.stack([_vecT(b_mod[L], 48) for L in range(2)]),
        "gmixT": np.stack([_vecT(g_mix[L], 8) for L in range(2)]),
        "gffnT": np.stack([_vecT(g_ffn[L], 8) for L in range(2)]),
        "gfinT": _vecT(g_final, 8),
        "bgateT": np.stack([_vecT(b_gate[L], 24) for L in range(2)]),
        "sinkx": np.ascontiguousarray(sink.astype(f).reshape(1, 32)),
        "gvb": np.ascontiguousarray(np.broadcast_to(g_v.astype(f)[:, None, :], (2, 128, 1024))),
        "maskP": (jj >= pp).astype(ml_dtypes.bfloat16),
        "maskN": (jj <= pp).astype(ml_dtypes.bfloat16),
        "permM": _perm_matrix(),
        "tape32": tape,
    }
    mirror = []
    for h in range(2):
        ws = w_spatial.astype(f)
        bs = b_spatial.astype(f)
        sc = w_sconv.astype(f)
        fc = w_fconv.astype(f)
        if h == 1:
            ws = ws[:, :, ::-1, ::-1]
            bs = bs[:, :, ::-1]
            sc = sc[:, ::-1, :]
            fc = fc[:, ::-1, :]
        wsT = np.ascontiguousarray(ws.transpose(0, 3, 1, 2)).reshape(2, 128, 1024)
        bsrow = np.ascontiguousarray(bs).reshape(2, 1, 1024)
        wsconvT = np.ascontiguousarray(sc.reshape(2, 3, 8, 128).transpose(0, 3, 2, 1)).reshape(2, 128, 24)
        wfconvT = np.ascontiguousarray(fc.reshape(2, 3, NJ, 128).transpose(0, 3, 2, 1)).reshape(2, 128, 66)
        mirror.append({"wsT": wsT, "bsrow": bsrow, "wsconvT": wsconvT, "wfconvT": wfconvT})
    return {"common": common, "mirror": mirror}


_NC_CACHE = {}


def kernel(x, c, ctx, c_ctx, w_mod, b_mod, g_mix, w_in, b_gate, sink, w_spatial, b_spatial, g_v, w_sconv,
           w_branch, w_out, g_ffn, w_up, w_fconv, w_down, g_final, _debug=False, _upto=None, _ncores=8):
    args = [np.asarray(a, dtype=np.float32) for a in (x, c, ctx, c_ctx, w_mod, b_mod, g_mix, w_in, b_gate, sink,
                                                        w_spatial, b_spatial, g_v, w_sconv, w_branch, w_out, g_ffn,
                                                        w_up, w_fconv, w_down, g_final)]
    (x, c, ctx, c_ctx, w_mod, b_mod, g_mix, w_in, b_gate, sink, w_spatial, b_spatial, g_v, w_sconv, w_branch,
     w_out, g_ffn, w_up, w_fconv, w_down, g_final) = args
    shared = _prep_shared(w_mod, b_mod, g_mix, w_in, b_gate, sink, w_spatial, b_spatial, g_v, w_sconv, w_branch,
                          w_out, g_ffn, w_up, w_fconv, w_down, g_final)
    in_maps = []
    for core in range(_ncores):
        b, h = core // 2, core % 2
        in_maps.append(_prep_core_inputs(b, h, x, c, ctx, c_ctx, shared))
    key = (bool(_debug), _upto)
    if key not in _NC_CACHE:
        _NC_CACHE[key] = build_program(debug=_debug, upto=_upto)
    nc = _NC_CACHE[key]
    res = run_bass_kernel_spmd(nc, in_maps, core_ids=list(range(_ncores)))
    out = np.zeros((4, SEQ, D), np.float32)
    for core in range(_ncores):
        b, h = core // 2, core % 2
        oT = np.asarray(res.results[core]["outT"]).reshape(1024, HALF)
        if h == 0:
            out[b, 0:HALF, :] = oT.T
        else:
            out[b, SEQ - 1 - np.arange(HALF), :] = oT.T
    if _debug:
        return out, res
    return out
```
